# Optimizing a Trainium2 kernel written in Bass

```python
import jax, jax.numpy as jnp
from jax import lax
import numpy as np

D_MODEL = 1024
BATCH = 8
SEQ = 4096
DEPTH = 1
DEC_BATCH = 8
DEC_SEQ = 16
PAST_LEN = 4096

CHUNK = 64
D_MIX = D_MODEL
RET_HEADS = 4
RET_DK = 128
RET_DV = 128
RET_WIDTH = RET_HEADS * RET_DV
CONV_WIDTH = D_MIX - RET_WIDTH
CONV_GROUPS = 8
CONV_K = 3
D_FF = 4 * D_MODEL
FFN_CONV_K = 3
ROPE_BASE = 10000.0
EPS = 1e-6
IN_COLS = 3 * RET_HEADS * RET_DK + RET_WIDTH + 3 * CONV_WIDTH

kernel_name = "hybrid_retention_shortconv_streaming_step"


def rms_norm(x, w):
    xf = x.astype(jnp.float32)
    var = jnp.mean(xf * xf, axis=-1, keepdims=True)
    return (xf * lax.rsqrt(var + EPS)).astype(x.dtype) * w


def rotary(x, pos):
    d = x.shape[-1]
    inv = ROPE_BASE ** (-jnp.arange(0, d, 2, dtype=jnp.float32) / d)
    ang = pos.astype(jnp.float32)[:, None] * inv[None, :]
    cos = jnp.cos(ang)[:, None, :]
    sin = jnp.sin(ang)[:, None, :]
    xf = x.astype(jnp.float32)
    x1, x2 = xf[..., : d // 2], xf[..., d // 2:]
    return jnp.concatenate([x1 * cos - x2 * sin, x2 * cos + x1 * sin], axis=-1).astype(x.dtype)


def causal_dwconv(x, buf, w):
    k = w.shape[0]
    L = x.shape[1]
    xp = jnp.concatenate([buf.astype(x.dtype), x], axis=1)
    y = sum(xp[:, i:i + L] * w[i] for i in range(k))
    return y, xp[:, L:]


def retention_chunkwise(q, k, v, s0):
    C = q.shape[2]
    dt = q.dtype
    lg = jnp.log(1.0 - 2.0 ** (-5.0 - jnp.arange(RET_HEADS, dtype=jnp.float32)))
    idx = jnp.arange(C, dtype=jnp.float32)
    dmat = jnp.exp(lg[:, None, None] * jnp.abs(idx[:, None] - idx[None, :])).astype(dt)
    to_end = jnp.exp(lg[None, :] * (C - 1 - idx)[:, None]).astype(dt)
    from_start = jnp.exp(lg[None, :] * (idx + 1)[:, None]).astype(dt)
    g_chunk = jnp.exp(lg * C).astype(dt)[:, None, None]

    scores = jnp.einsum('bnihd,bnjhd->bnhij', q, k) * dmat
    intra = jnp.einsum('bnhij,bnjhe->bnihe', scores, v)
    upd = jnp.einsum('bnjhd,bnjhe->bnhde', k * to_end[None, None, :, :, None], v)

    def step(s, u):
        return g_chunk * s + u, s

    s_final, s_prev = lax.scan(step, s0.astype(upd.dtype), jnp.moveaxis(upd, 1, 0))
    s_prev = jnp.moveaxis(s_prev, 0, 1)
    cross = jnp.einsum('bnihd,bnhde->bnihe', q, s_prev) * from_start[None, None, :, :, None]
    return intra + cross, s_final


def layer(x, pos, chunk, s_ret, c_conv, c_ffn,
          w_in, w_out, conv_w, ret_norm_w, pre_mix_w, post_mix_w,
          pre_ffn_w, post_ffn_w, w_up, w_gate, ffn_conv_w, w_down):
    b, L, _ = x.shape
    n_chunks = L // chunk
    h = rms_norm(x, pre_mix_w)
    p = h @ w_in
    qd = RET_HEADS * RET_DK
    q, k, v, g, bg, cg, hc = jnp.split(
        p, np.cumsum([qd, qd, RET_WIDTH, RET_WIDTH, CONV_WIDTH, CONV_WIDTH]).tolist(), axis=-1)

    q = rotary(q.reshape(b, L, RET_HEADS, RET_DK), pos)
    k = rotary(k.reshape(b, L, RET_HEADS, RET_DK), pos) * (RET_DK ** -0.5)
    v = v.reshape(b, L, RET_HEADS, RET_DV)
    o, s_new = retention_chunkwise(
        q.reshape(b, n_chunks, chunk, RET_HEADS, RET_DK),
        k.reshape(b, n_chunks, chunk, RET_HEADS, RET_DK),
        v.reshape(b, n_chunks, chunk, RET_HEADS, RET_DV), s_ret)
    o = rms_norm(o.reshape(b, L, RET_HEADS, RET_DV), ret_norm_w.reshape(RET_HEADS, RET_DV))
    o = o.reshape(b, L, RET_WIDTH) * jax.nn.silu(g)

    yc, c_conv_new = causal_dwconv(cg * hc, c_conv, conv_w)
    yc = bg * yc

    x = x + rms_norm(jnp.concatenate([o, yc], axis=-1) @ w_out, post_mix_w)

    h2 = rms_norm(x, pre_ffn_w)
    u, c_ffn_new = causal_dwconv(h2 @ w_up, c_ffn, ffn_conv_w)
    f = (jax.nn.gelu(u, approximate=True) * (h2 @ w_gate)) @ w_down
    x = x + rms_norm(f, post_ffn_w)
    return x, s_new.astype(x.dtype), c_conv_new, c_ffn_new


def setup_inputs(seed: int = 0) -> dict:
    key = jax.random.key(seed)
    ks = jax.random.split(key, 20)
    nrm = lambda k, s, sc: jax.random.normal(k, s, jnp.float32) * sc
    gain = lambda k, n: 1.0 + nrm(k, (DEPTH, n), 0.05)
    return {
        "x_prompt": nrm(ks[0], (BATCH, SEQ, D_MODEL), 1.0),
        "x_sample": nrm(ks[1], (DEC_BATCH, DEC_SEQ, D_MODEL), 1.0),
        "state_ret": nrm(ks[2], (DEPTH, DEC_BATCH, RET_HEADS, RET_DK, RET_DV), 0.05),
        "cache_conv": nrm(ks[3], (DEPTH, DEC_BATCH, CONV_K - 1, CONV_WIDTH), 1.0),
        "cache_ffn_conv": nrm(ks[4], (DEPTH, DEC_BATCH, FFN_CONV_K - 1, D_FF), 1.0),
        "w_in": nrm(ks[5], (DEPTH, D_MODEL, IN_COLS), D_MODEL ** -0.5),
        "w_out": nrm(ks[6], (DEPTH, D_MIX, D_MODEL), D_MIX ** -0.5),
        "conv_w": nrm(ks[7], (DEPTH, CONV_K, CONV_WIDTH), CONV_K ** -0.5),
        "ret_norm_w": gain(ks[8], RET_WIDTH),
        "pre_mix_w": gain(ks[9], D_MODEL),
        "post_mix_w": gain(ks[10], D_MODEL),
        "pre_ffn_w": gain(ks[11], D_MODEL),
        "post_ffn_w": gain(ks[12], D_MODEL),
        "w_up": nrm(ks[13], (DEPTH, D_MODEL, D_FF), D_MODEL ** -0.5),
        "w_gate": nrm(ks[14], (DEPTH, D_MODEL, D_FF), D_MODEL ** -0.5),
        "ffn_conv_w": nrm(ks[15], (DEPTH, FFN_CONV_K, D_FF), FFN_CONV_K ** -0.5),
        "w_down": nrm(ks[16], (DEPTH, D_FF, D_MODEL), D_FF ** -0.5),
    }


def reference(x_prompt, x_sample, state_ret, cache_conv, cache_ffn_conv,
              w_in, w_out, conv_w, ret_norm_w, pre_mix_w, post_mix_w,
              pre_ffn_w, post_ffn_w, w_up, w_gate, ffn_conv_w, w_down):
    bp, lp, _ = x_prompt.shape
    bs, ls, _ = x_sample.shape
    dt = x_prompt.dtype
    pos_p = jnp.arange(lp)
    pos_s = PAST_LEN + jnp.arange(ls)
    zs_ret = jnp.zeros((bp, RET_HEADS, RET_DK, RET_DV), dt)
    zs_conv = jnp.zeros((bp, CONV_K - 1, CONV_WIDTH), dt)
    zs_ffn = jnp.zeros((bp, FFN_CONV_K - 1, D_FF), dt)

    yp, ys = x_prompt, x_sample
    sr_p, cc_p, cf_p, sr_s, cc_s, cf_s = [], [], [], [], [], []
    for l in range(DEPTH):
        w = (w_in[l], w_out[l], conv_w[l], ret_norm_w[l], pre_mix_w[l], post_mix_w[l],
             pre_ffn_w[l], post_ffn_w[l], w_up[l], w_gate[l], ffn_conv_w[l], w_down[l])
        yp, a, b_, c = layer(yp, pos_p, CHUNK, zs_ret, zs_conv, zs_ffn, *w)
        sr_p.append(a); cc_p.append(b_); cf_p.append(c)
        ys, a, b_, c = layer(ys, pos_s, ls, state_ret[l], cache_conv[l], cache_ffn_conv[l], *w)
        sr_s.append(a); cc_s.append(b_); cf_s.append(c)

    return (yp, ys,
            jnp.stack(sr_p), jnp.stack(cc_p), jnp.stack(cf_p),
            jnp.stack(sr_s), jnp.stack(cc_s), jnp.stack(cf_s))
```

```python
import math
import numpy as np
import concourse.bass as bass
import concourse.mybir as mybir
from concourse.bass_utils import run_bass_kernel_spmd

F32 = mybir.dt.float32
BF16 = mybir.dt.bfloat16
I32 = mybir.dt.int32
AF = mybir.ActivationFunctionType
ALU = mybir.AluOpType

D = 1024
SEQ = 4096
DEC = 16
PAST_LEN = 4096
TT = 512
NT_PROMPT = SEQ // TT
H = 4
DFF = 4096
EPS = 1e-6
NCORES = 8
LG = [math.log(1.0 - 2.0 ** (-5.0 - h)) for h in range(H)]
SC = 128.0 ** -0.5
TWO_PI = 2.0 * math.pi


GROUP_KEYS = ("castug", "cst", "cstslow", "cst2", "so")


class Prog:
    def __init__(self, nc):
        self.nc = nc
        self.ops = []
        self.last_w = {}
        self.readers = {}
        self.out_dma = []

    def add(self, eng, fn, reads=(), writes=(), dma_key=None, is_out=False):
        idx = len(self.ops)
        deps = set()
        for r in reads:
            if r in self.last_w:
                deps.add(self.last_w[r])
        for w in writes:
            if w in self.last_w:
                deps.add(self.last_w[w])
            for rd in self.readers.get(w, {}).values():
                deps.add(rd)
        for w in writes:
            self.last_w[w] = idx
            self.readers[w] = {}
        for r in reads:
            if r not in writes:
                rk = eng if dma_key is None else ("dma", idx)
                self.readers.setdefault(r, {})[rk] = idx
        deps.discard(idx)
        if dma_key is not None and dma_key[0] in GROUP_KEYS:
            deps = {d for d in deps if self.ops[d]["dma_key"] != dma_key}
        self.ops.append(dict(eng=eng, fn=fn, deps=deps, dma_key=dma_key, idx=idx))
        if is_out:
            self.out_dma.append(idx)
        return idx

    def emit(self):
        nc = self.nc
        ops = self.ops
        needed = set()
        for o in ops:
            for d in o["deps"]:
                p = ops[d]
                if p["eng"] == "pe" and o["eng"] == "pe" and p["dma_key"] is None and o["dma_key"] is None:
                    continue
                needed.add(d)
        eng_names = ["pe", "act", "dve", "pool", "sp"]
        counts = {e: 0 for e in eng_names}
        dma_counts = {}
        dma_keys = []
        for o in ops:
            if o["dma_key"] is not None:
                k = o["dma_key"]
                if k not in dma_counts:
                    dma_counts[k] = 0
                    dma_keys.append(k)
                dma_counts[k] += 16
                o["tok"] = (("dma", k), dma_counts[k])
                o["is_dma"] = True
            elif o["idx"] in needed:
                counts[o["eng"]] += 1
                o["tok"] = (("eng", o["eng"]), counts[o["eng"]])
            else:
                o["tok"] = None
        for o in ops:
            if o["dma_key"] is not None and o["dma_key"][0] in GROUP_KEYS:
                o["tok"] = (("dma", o["dma_key"]), dma_counts[o["dma_key"]])
        import contextlib
        with contextlib.ExitStack() as es:
            sems = {}
            for e in eng_names:
                sems[("eng", e)] = es.enter_context(nc.semaphore("s_" + e))
            for i, k in enumerate(dma_keys):
                sems[("dma", k)] = es.enter_context(nc.semaphore("d%d" % i))
            block = es.enter_context(nc.Block())
            per_eng = {e: [o for o in ops if o["eng"] == e] for e in eng_names}
            final_waits = []
            for k in dma_keys:
                final_waits.append((("dma", k), dma_counts[k]))

            def run(engname, eng, is_last=False):
                waited = {}
                for o in per_eng[engname]:
                    need = {}
                    for d in o["deps"]:
                        p = ops[d]
                        if p["tok"] is None:
                            continue
                        if p["eng"] == "pe" and engname == "pe" and p["dma_key"] is None and o["dma_key"] is None:
                            continue
                        sk, val = p["tok"]
                        if need.get(sk, 0) < val:
                            need[sk] = val
                    todo = [(sk, val) for sk, val in need.items() if waited.get(sk, 0) < val]
                    for sk, val in todo[:-1]:
                        eng.wait_ge(sems[sk], val)
                        waited[sk] = val
                    inst = o["fn"](eng)
                    if todo:
                        sk, val = todo[-1]
                        inst._wait_ge(sems[sk], val)
                        waited[sk] = val
                    if o["tok"] is not None:
                        sk, val = o["tok"]
                        inst.then_inc(sems[sk], 16 if sk[0] == "dma" else 1)
                if is_last:
                    for sk, val in final_waits:
                        eng.wait_ge(sems[sk], val)

            @block.tensor
            def _(e):
                run("pe", e)

            @block.scalar
            def _(e):
                run("act", e)

            @block.vector
            def _(e):
                run("dve", e)

            @block.gpsimd
            def _(e):
                run("pool", e)

            @block.sync
            def _(e):
                run("sp", e, is_last=True)


def build_program():
    nc = bass.Bass("TRN2", target_bir_lowering=False)
    P = Prog(nc)

    def din(name, shape):
        return nc.dram_tensor(name, shape, F32, kind="ExternalInput").ap()

    def dout(name, shape):
        return nc.dram_tensor(name, shape, F32, kind="ExternalOutput").ap()

    xp = din("xp", [SEQ, D]); xs = din("xs", [DEC, D])
    st_in = din("st", [H, 128, 128]); cc_in = din("cc", [2, 512]); cf_in = din("cf", [2, DFF])
    w_in = din("w_in", [D, 3584]); w_out = din("w_out", [D, D])
    conv_w = din("conv_w", [3, 512]); ret_w = din("ret_w", [1, 512])
    pre_mix = din("pre_mix", [1, D]); post_mix = din("post_mix", [1, D])
    pre_ffn = din("pre_ffn", [1, D]); post_ffn = din("post_ffn", [1, D])
    w_up = din("w_up", [D, DFF]); w_gate = din("w_gate", [D, DFF])
    ffn_cw = din("ffn_cw", [3, DFF]); w_down = din("w_down", [DFF, D])
    yp = dout("yp", [SEQ, D]); ys = dout("ys", [DEC, D])
    srp = dout("srp", [H, 128, 128]); ccp = dout("ccp", [2, 512]); cfp = dout("cfp", [2, DFF])
    srs = dout("srs", [H, 128, 128]); ccs = dout("ccs", [2, 512]); cfs = dout("cfs", [2, DFF])

    win_s = nc.dram_tensor("win_s", [7, 128, 8, 512], BF16).ap()
    wout_s = nc.dram_tensor("wout_s", [128, 8, 1024], BF16).ap()
    wug_s = nc.dram_tensor("wug_s", [32, 128, 2, 8, 128], BF16).ap()
    wdn_s = nc.dram_tensor("wdn_s", [4, 128, 8, 1024], BF16).ap()

    def sb(name, shape, dt):
        return nc.alloc_sbuf_tensor(name, shape, dt).ap()

    x_sb = sb("x_sb", [128, 5, D], F32)
    hT = sb("hT", [128, 8, TT], BF16)
    wslab = sb("wslab", [128, 2, 8, 512], BF16)
    f_acc = sb("f_acc", [128, 4, D], F32)
    wout_sb = f_acc.bitcast(BF16).rearrange("p a (b c) -> p (a b) c", c=1024)
    NWUG = 3
    wug_sb = sb("wug_sb", [128, NWUG, 2, 8, 128], BF16)
    qT = sb("qT", [128, H, TT], BF16); kT = sb("kT", [128, H, TT], BF16)
    kte = sb("kte", [128, 4, 512], BF16); v_sb = sb("v_sb", [128, 4, 512], BF16); gs = sb("gs", [128, 4, 512], BF16)
    qrot = sb("qrot", [128, 2, 512], BF16)
    tmpA = sb("tmpA", [128, 1024], F32)
    z = sb("z", [128, 4, TT + 2], F32)
    hc_sb = sb("hc_sb", [128, 2, TT], F32); yc = sb("yc", [128, 2, TT], F32)
    mixT = sb("mixT", [128, 8, TT], BF16)
    up_sb = sb("up_sb", [128, 2, TT + 2], F32); acc = sb("acc", [128, 2, TT], F32); gl = sb("gl", [128, 2, TT], BF16)
    act = sb("act", [128, 8, TT], BF16)
    wdn_sb = sb("wdn_sb", [128, 8, 1024], BF16)
    wb_pre_mix = sb("wb_pre_mix", [128, D], F32); wb_post_mix = sb("wb_post_mix", [128, D], F32)
    wb_pre_ffn = sb("wb_pre_ffn", [128, D], F32); wb_post_ffn = sb("wb_post_ffn", [128, D], F32)
    wb_ret = sb("wb_ret", [128, 512], F32)
    cosT = sb("cosT", [128, 33, 64], F32); sinT = sb("sinT", [128, 33, 64], F32)
    mask2 = sb("mask2", [128, H, 128], F32)
    mask2s = sb("mask2s", [128, H, DEC], F32)
    ident_f = sb("ident_f", [128, 128], F32)
    te_t = sb("te_t", [128, 2, H], F32); epsfs = sb("epsfs", [128, 2, H], F32)
    S = sb("S", [128, H, 128], F32); S_bf = sb("S_bf", [128, 2, H, 128], BF16)
    Pm = sb("Pm", [128, H, 128], BF16)
    ret_b = sb("ret_b", [128, 2, 512], BF16)
    hb = sb("hb", [128, 2, D], BF16)
    halo_f = sb("halo_f", [128, 32, 2], F32)
    z_s = sb("z_s", [128, 4, DEC + 2], F32)
    halo_s = sb("halo_s", [128, 32, 2], F32)
    cur = {"z": z, "zk": "z", "hf": halo_f, "hfk": "halo_f", "xs": [0, 1, 2, 3]}
    cw = sb("cw", [128, 4, 3], F32); fw = sb("fw", [128, 32, 3], F32)
    ident = sb("ident", [128, 128], BF16)
    negh = sb("negh", [128, 8], F32)
    stat = sb("stat", [128, 96], F32)
    inv_f = sb("inv_f", [128, 64], F32)
    ptmp = wdn_sb.bitcast(F32).rearrange("p a c -> p (a c)")

    def ps(name, shape, dt):
        return nc.alloc_psum_tensor(name, shape, dt).ap()

    p34 = ps("p34", [128, 1024], F32)
    pb = [ps("pb0", [128, 512], F32), ps("pb1", [128, 512], F32), None,
          p34[:, 0:512], p34[:, 512:1024], ps("pb5", [128, 512], F32)]
    pT = ps("pT", [128, 1024], BF16)
    p67 = ps("p67", [128, 1024], F32)

    def dma(q, out, in_, reads, writes, key, is_out=False, slow=False):
        def fn(e, out=out, in_=in_):
            if slow:
                return e.dma_start(out=out, in_=in_, allow_slow_non_contiguous=True)
            return e.dma_start(out=out, in_=in_)
        P.add(q, fn, reads, writes, dma_key=key, is_out=is_out)

    def op(eng, f, reads, writes):
        reads = list(reads)
        if any(k[0] == "pt" for k in list(reads) + list(writes)):
            reads.append(("wdn_sb",))
        P.add(eng, f, reads, writes)

    stat_ctr = [0]

    def stat_col(n=1):
        assert n <= 12
        k = stat_ctr[0] % 8
        stat_ctr[0] += 1
        return 12 * k

    w_in_v = w_in.rearrange("(k p) (g c) -> g p k c", p=128, c=512)
    w_up_v = w_up.rearrange("(k p) (c j) -> c p k j", p=128, j=128)
    w_gate_v = w_gate.rearrange("(k p) (c j) -> c p k j", p=128, j=128)
    w_dn_v = w_down.rearrange("(g cc p) n -> g p cc n", cc=8, p=128)
    cast_queue = []
    for g in range(7):
        cast_queue.append(lambda g=g: dma("pool", win_s[g], w_in_v[g], [], [("win_s", g)], ("castx", "in", g)))
    cast_queue.append(lambda: dma("pool", wout_s, w_out.rearrange("(k p) c -> p k c", p=128), [], [("wout_s",)], ("castx", "out")))
    for g in range(4):
        for c2 in range(8):
            c0 = g * 8 + c2
            cast_queue.append(lambda c0=c0, g=g: dma("pool", wug_s[c0, :, 0], w_up_v[c0], [], [("wug_s", c0, 0)], ("castug", c0)))
            cast_queue.append(lambda c0=c0, g=g: dma("pool", wug_s[c0, :, 1], w_gate_v[c0], [], [("wug_s", c0, 1)], ("castug", c0)))
        cast_queue.append(lambda g=g: dma("pool", wdn_s[g], w_dn_v[g], [], [("wdn_s", g)], ("castx", "d", g)))

    def issue_casts(n):
        for _ in range(n):
            if cast_queue:
                cast_queue.pop(0)()

    dma("sp", wb_pre_mix, pre_mix.partition_broadcast(128), [], [("wb_pre_mix",)], ("cst",))
    dma("sp", wb_post_mix, post_mix.partition_broadcast(128), [], [("wb_post_mix",)], ("cst",))
    dma("sp", wb_pre_ffn, pre_ffn.partition_broadcast(128), [], [("wb_pre_ffn",)], ("cst",))
    dma("sp", wb_post_ffn, post_ffn.partition_broadcast(128), [], [("wb_post_ffn",)], ("cst",))
    dma("sp", wb_ret, ret_w.partition_broadcast(128), [], [("wb_ret",)], ("cst",))
    def load_slow_consts():
        for i in range(3):
            dma("sp", cw[:, :, i], conv_w[i].rearrange("(j p) -> p j", p=128), [], [("cw",)], ("cstslow",), slow=True)
        for i in range(3):
            dma("sp", fw[:, :, i], ffn_cw[i].rearrange("(c p) -> p c", p=128), [], [("fw",)], ("cstslow",), slow=True)
        for r in range(2):
            dma("sp", z_s[:, :, r], cc_in[r].rearrange("(j p) -> p j", p=128), [], [("z_s", j) for j in range(4)], ("cstslow",), slow=True)
        for r in range(2):
            dma("sp", halo_s[:, :, r], cf_in[r].rearrange("(c p) -> p c", p=128), [], [("halo_s",)], ("cstslow",), slow=True)

    op("dve", lambda e: e.memset(negh, -0.5), [], [("negh",)])
    idi = ptmp[:, 0:128].bitcast(I32)
    idf = ptmp[:, 128:256]
    op("pool", lambda e: e.iota(idi, pattern=[[1, 128]], base=0, channel_multiplier=-1), [], [("pt", 0)])
    op("dve", lambda e: e.tensor_copy(out=idf, in_=idi), [("pt", 0)], [("pt", 1)])
    op("dve", lambda e: e.tensor_scalar(out=ident, in0=idf, scalar1=0.0, scalar2=None, op0=ALU.is_equal), [("pt", 1)], [("ident",)])
    op("dve", lambda e: e.tensor_scalar(out=ident_f, in0=idf, scalar1=0.0, scalar2=None, op0=ALU.is_equal), [("pt", 1)], [("ident_f",)])

    def build_tables():
        dji = ptmp[:, 256:384].bitcast(I32)
        djf = ptmp[:, 384:512]
        ef = ptmp[:, 512:640]
        pidx_i = ptmp[:, 640:641].bitcast(I32)
        pidx_f = ptmp[:, 641:642]
        tv = ptmp[:, 642:643]
        op("pool", lambda e: e.iota(dji, pattern=[[-1, 128]], base=0, channel_multiplier=1), [], [("pt", 2)])
        op("dve", lambda e: e.tensor_copy(out=djf, in_=dji), [("pt", 2)], [("pt", 3)])
        op("dve", lambda e: e.tensor_scalar(out=djf, in0=djf, scalar1=0.0, scalar2=2.0, op0=ALU.max, op1=ALU.mult), [("pt", 3)], [("pt", 3)])
        op("pool", lambda e: e.iota(pidx_i, pattern=[[0, 1]], base=0, channel_multiplier=1), [], [("pt", 4)])
        op("dve", lambda e: e.tensor_copy(out=pidx_f, in_=pidx_i), [("pt", 4)], [("pt", 5)])
        for ti, L in ((0, 16), (1, 128)):
            op("dve", lambda e, L=L: e.tensor_scalar(out=ef, in0=djf, scalar1=float(-L), scalar2=None, op0=ALU.add), [("pt", 3)], [("pt", 6)])
            for h in range(H):
                if ti == 0:
                    op("act", lambda e, h=h: e.activation(out=mask2s[:, h, :], in_=ef[:, 0:DEC], func=AF.Exp, scale=LG[h]), [("pt", 6)], [("mask2",)])
                else:
                    op("act", lambda e, h=h: e.activation(out=mask2[:, h, :], in_=ef, func=AF.Exp, scale=LG[h]), [("pt", 6)], [("mask2",)])
            op("dve", lambda e, L=L: e.tensor_scalar(out=tv, in0=pidx_f, scalar1=-1.0, scalar2=float(L - 1), op0=ALU.mult, op1=ALU.add), [("pt", 5)], [("pt", 7)])
            for h in range(H):
                op("act", lambda e, ti=ti, h=h: e.activation(out=te_t[:, ti, h:h + 1], in_=tv, func=AF.Exp, scale=LG[h]), [("pt", 7)], [("te_t",)])
            op("dve", lambda e: e.tensor_scalar(out=tv, in0=pidx_f, scalar1=1.0, scalar2=None, op0=ALU.add), [("pt", 5)], [("pt", 7)])
            for h in range(H):
                op("act", lambda e, ti=ti, h=h: e.activation(out=epsfs[:, ti, h:h + 1], in_=tv, func=AF.Exp, scale=-2.0 * LG[h]), [("pt", 7)], [("epsfs",)])
        op("dve", lambda e: e.tensor_scalar(out=te_t, in0=te_t, scalar1=SC, scalar2=None, op0=ALU.mult), [("te_t",)], [("te_t",)])
        op("dve", lambda e: e.tensor_scalar(out=epsfs, in0=epsfs, scalar1=EPS, scalar2=None, op0=ALU.mult), [("epsfs",)], [("epsfs",)])
        op("dve", lambda e: e.memset(mask2[64:128, :, 0:64], 0.0), [("mask2",)], [("mask2",)])

        for i in range(64):
            op("dve", lambda e, i=i: e.memset(inv_f[:, i:i + 1], float(np.float32(10000.0 ** (-(2.0 * i) / 128.0)))), [], [("inv_f",)])
        pos_i = ptmp[:, 700:733].bitcast(I32)
        pos_f = ptmp[:, 740:773]
        op("pool", lambda e: e.iota(pos_i[:, 1:33], pattern=[[128, 32]], base=0, channel_multiplier=1), [], [("pt", 8)])
        op("pool", lambda e: e.iota(pos_i[:, 0:1], pattern=[[0, 1]], base=PAST_LEN, channel_multiplier=1), [("pt", 8)], [("pt", 8)])
        op("dve", lambda e: e.tensor_copy(out=pos_f, in_=pos_i), [("pt", 8)], [("pt", 9)])
        NB = 11
        ang = ptmp[:, 1024:1024 + NB * 64].rearrange("p (n i) -> p n i", i=64)
        tq = ptmp[:, 1792:1792 + NB * 64].rearrange("p (n i) -> p n i", i=64)
        ki = ptmp[:, 2560:2560 + NB * 64].bitcast(I32).rearrange("p (n i) -> p n i", i=64)
        rr = ptmp[:, 3328:3328 + NB * 64].rearrange("p (n i) -> p n i", i=64)
        for blk in range(3):
            n0 = blk * NB
            R_ANG, R_TQ, R_KI, R_RR = ("pt", 10), ("pt", 11), ("pt", 12), ("pt", 13)
            op("dve", lambda e, n0=n0: e.tensor_tensor(out=ang, in0=pos_f[:, n0:n0 + NB].unsqueeze(2).broadcast_to([128, NB, 64]),
                                                       in1=inv_f.unsqueeze(1).broadcast_to([128, NB, 64]), op=ALU.mult),
               [("pt", 9), ("inv_f",)], [R_ANG])
            op("dve", lambda e: e.tensor_scalar(out=tq, in0=ang, scalar1=1.0 / TWO_PI, scalar2=None, op0=ALU.mult), [R_ANG], [R_TQ])
            op("dve", lambda e: e.tensor_copy(out=ki, in_=tq), [R_TQ], [R_KI])
            op("dve", lambda e: e.tensor_copy(out=tq, in_=ki), [R_KI], [R_TQ])
            op("dve", lambda e: e.scalar_tensor_tensor(out=rr, in0=tq, scalar=-TWO_PI, in1=ang, op0=ALU.mult, op1=ALU.add), [R_TQ, R_ANG], [R_RR])
            op("dve", lambda e: e.tensor_scalar(out=tq, in0=rr, scalar1=math.pi, scalar2=-math.pi, op0=ALU.min, op1=ALU.max), [R_RR], [R_TQ])
            op("act", lambda e, n0=n0: e.activation(out=sinT[:, n0:n0 + NB, :], in_=tq, func=AF.Sin), [R_TQ], [("sinT",)])
            op("dve", lambda e: e.tensor_scalar(out=ang, in0=rr, scalar1=math.pi / 2, scalar2=None, op0=ALU.add), [R_RR], [R_ANG])
            op("dve", lambda e: e.tensor_scalar(out=tq, in0=ang, scalar1=math.pi, scalar2=-TWO_PI, op0=ALU.is_gt, op1=ALU.mult), [R_ANG, ("sinT",)], [R_TQ])
            op("dve", lambda e: e.tensor_tensor(out=rr, in0=ang, in1=tq, op=ALU.add), [R_ANG, R_TQ], [R_RR])
            op("dve", lambda e: e.tensor_scalar(out=rr, in0=rr, scalar1=math.pi, scalar2=-math.pi, op0=ALU.min, op1=ALU.max), [R_RR], [R_RR])
            op("act", lambda e, n0=n0: e.activation(out=cosT[:, n0:n0 + NB, :], in_=rr, func=AF.Sin), [R_RR], [("cosT",)])

    tables_built = [False]

    pe_ctr = {"pa": 0, "rot": 0, "sbf": 0, "wug": 0, "retb": 0, "dn": 0}
    pending = []
    wug_slot_of = {}
    tmpA_b = tmpA.bitcast(BF16)
    RTMP_ALL = [("tmpA", b, i) for b in range(2) for i in range(3)]

    def flush_pending(keep=0):
        while len(pending) > keep:
            pending.pop(0)()

    def norm_stages(t, s, m, wb_tile, wb_key, xslot=None, sq_qrot=False):
        c = stat_col(3)
        ssa = stat[:m, c:c + 1]; aa = stat[:m, c + 1:c + 2]; ra = stat[:m, c + 2:c + 3]
        RS = ("stat", c)
        hbi = s % 2
        xs_ = cur["xs"][s] if xslot is None else xslot

        def st0():
            if sq_qrot:
                op("act", lambda e: e.activation(out=qrot.rearrange("p a c -> p (a c)")[:m, :], in_=x_sb[:m, xs_, :], func=AF.Square, accum_out=ssa),
                   [("x", xs_), RS], [RS] + [("qrot", a, b) for a in range(2) for b in range(2)])
            else:
                op("act", lambda e: e.activation(out=hb[:m, hbi, :], in_=x_sb[:m, xs_, :], func=AF.Square, accum_out=ssa), [("x", xs_), RS], [RS, ("hb", hbi)])

        def st1():
            op("dve", lambda e: e.tensor_scalar(out=aa, in0=ssa, scalar1=1.0 / D, scalar2=EPS, op0=ALU.mult, op1=ALU.add), [RS], [RS])

        def st2():
            op("pool", lambda e: e.tensor_tensor(out=ra, in0=aa, in1=negh[:m, 0:1], op=ALU.pow), [RS, ("negh",)], [RS])

        def st3():
            op("dve", lambda e: e.scalar_tensor_tensor(out=hb[:m, hbi, :], in0=x_sb[:m, xs_, :], scalar=ra, in1=wb_tile[:m, :],
                                                       op0=ALU.mult, op1=ALU.mult),
               [("x", xs_), RS, wb_key], [("hb", hbi)])

        def part_b():
            RT = ("pT",)
            for k in range(8):
                op("pe", lambda e, k=k: e.transpose(out=pT[:, k * 128: k * 128 + m], in_=hb[:m, hbi, k * 128:(k + 1) * 128], identity=ident[:m, :m]),
                   [("hb", hbi), ("ident",)], [RT])
            src = pT.rearrange("p (k c) -> p k c", c=128)[:, :, :m]
            dst = hT[:, :, s * 128:s * 128 + m]
            op("dve", lambda e: e.tensor_copy(out=dst, in_=src), [RT], [("hT", s)])
        return [st0, st1, st2, st3, part_b]

    def norm_to_hT(t, s, m, wb_tile, wb_key):
        st = norm_stages(t, s, m, wb_tile, wb_key)
        for f in st[:4]:
            f()
        return st[4]

    def load_slab_into(g, slot_):
        dma("sp", wslab[:, slot_], win_s[g], [("win_s", g)], [("wslab", slot_)], ("wslab", slot_))
        return slot_

    def proj_tokmajor(t, g, slot, ns, msz, tabidx, pre_hooks=None):
        for s in range(ns):
            m = msz[s]
            if pre_hooks is not None:
                pre_hooks[s]()
            bank = pe_ctr["pa"] % 2; pe_ctr["pa"] += 1
            RB = ("pb", bank)
            for k in range(8):
                op("pe", lambda e, k=k, s=s, m=m, bank=bank: e.matmul(pb[bank][:m, :], lhsT=hT[:, k, s * 128:s * 128 + m],
                                                                       rhs=wslab[:, slot, k, :], start=(k == 0), stop=(k == 7)),
                   [("hT", s), ("wslab", slot)], [RB])
            flush_pending(keep=1)
            pq = pb[bank]
            if g in (0, 1):
                ti = tabidx[s]
                rb = pe_ctr["rot"] % 2; pe_ctr["rot"] += 1
                pqv = pq[:m, :].rearrange("p (h two d) -> p h two d", h=H, two=2)
                cb = cosT[:m, ti, :].unsqueeze(1).unsqueeze(1).broadcast_to([m, H, 2, 64])
                sbb = sinT[:m, ti, :].unsqueeze(1).broadcast_to([m, H, 64])
                t1 = tmpA_b[:m, rb * 1024:rb * 1024 + 512].rearrange("p (h two d) -> p h two d", h=H, two=2)
                t2 = tmpA_b[:m, rb * 1024 + 512:(rb + 1) * 1024].rearrange("p (h two d) -> p h two d", h=H, two=2)
                RT0, RT1, RT2 = ("tmpA", rb, 0), ("tmpA", rb, 1), ("tmpA", rb, 2)
                op("dve", lambda e, pqv=pqv, cb=cb, t1=t1: e.tensor_tensor(out=t1, in0=pqv, in1=cb, op=ALU.mult), [RB, ("cosT",)], [RT0])
                op("dve", lambda e, pqv=pqv, sbb=sbb, t2=t2: e.tensor_tensor(out=t2[:, :, 0, :], in0=pqv[:, :, 1, :], in1=sbb, op=ALU.mult),
                   [RB, ("sinT",)], [RT1])
                op("dve", lambda e, pqv=pqv, sbb=sbb, t2=t2: e.tensor_tensor(out=t2[:, :, 1, :], in0=pqv[:, :, 0, :], in1=sbb, op=ALU.mult),
                   [RB, ("sinT",)], [RT2])
                qi = rb
                qr = qrot[:m, qi, :].rearrange("p (h two d) -> p h two d", h=H, two=2)
                RQ0, RQ1 = ("qrot", qi, 0), ("qrot", qi, 1)
                op("pool", lambda e, t1=t1, t2=t2, qr=qr: e.tensor_tensor(out=qr[:, :, 0, :], in0=t1[:, :, 0, :], in1=t2[:, :, 0, :], op=ALU.subtract),
                   [RT0, RT1], [RQ0])
                op("pool", lambda e, t1=t1, t2=t2, qr=qr: e.tensor_tensor(out=qr[:, :, 1, :], in0=t1[:, :, 1, :], in1=t2[:, :, 1, :], op=ALU.add),
                   [RT0, RT2], [RQ1])
                if g == 1:
                    tix = 0 if t == 0 else 1
                    teb = te_t[:m, tix, :].unsqueeze(2).broadcast_to([m, H, 128])
                    kdst = kte[:m, s, :].rearrange("p (h d) -> p h d", h=H)
                    qsrc = qrot[:m, qi, :].rearrange("p (h d) -> p h d", h=H)
                    op("pool", lambda e, kdst=kdst, qsrc=qsrc, teb=teb: e.tensor_tensor(out=kdst, in0=qsrc, in1=teb, op=ALU.mult),
                       [RQ0, RQ1, ("te_t",)], [("kte", s)])
                    tsrc_all = kte[:m, s, :]
                    RSRC = [("kte", s)]
                    dstT = kT
                    RD = ("kT", s)
                else:
                    tsrc_all = qrot[:m, qi, :]
                    RSRC = [RQ0, RQ1]
                    dstT = qT
                    RD = ("qT", s)

                def tr(tsrc_all=tsrc_all, RSRC=RSRC, dstT=dstT, RD=RD, m=m, s=s):
                    RT = ("pT",)
                    for h in range(H):
                        op("pe", lambda e, h=h: e.transpose(out=pT[:, h * 128: h * 128 + m], in_=tsrc_all[:, h * 128:(h + 1) * 128], identity=ident[:m, :m]),
                           RSRC + [("ident",)], [RT])
                    src = pT[:, 0:512].rearrange("p (k c) -> p k c", c=128)[:, :, :m]
                    dst = dstT[:, :, s * 128:s * 128 + m]
                    op("act", lambda e: e.activation(out=dst, in_=src, func=AF.Copy), [RT], [RD])
                pending.append(tr)
            elif g == 2:
                op("act", lambda e, pq=pq, s=s, m=m: e.activation(out=v_sb[:m, s, :], in_=pq[:m, :], func=AF.Copy), [RB], [("v_sb", s)])
            else:
                op("act", lambda e, pq=pq, s=s, m=m: e.activation(out=gs[:m, s, :], in_=pq[:m, :], func=AF.Silu), [RB], [("gs", s)])
                op("pool", lambda e, s=s, m=m: e.tensor_tensor(out=gs[:m, s, :], in0=gs[:m, s, :], in1=wb_ret[:m, :], op=ALU.mult),
                   [("gs", s), ("wb_ret",)], [("gs", s)])

    def retention_parts(t, s, m):
        tix = 0 if t == 0 else 1
        cols = slice(s * 128, s * 128 + m)
        R3, R4, R5 = ("pb", 3), ("pb", 4), ("pb", 5)
        sb_cur = pe_ctr["sbf"] % 2; pe_ctr["sbf"] += 1
        sb_nxt = 1 - sb_cur
        c = stat_col(12)
        ssh = stat[:m, c:c + 4]; ah = stat[:m, c + 4:c + 8]; rh = stat[:m, c + 8:c + 12]
        RS = ("stat", c)
        rbi = pe_ctr["retb"] % 2; pe_ctr["retb"] += 1

        def part_a():
            for h in range(H):
                op("pe", lambda e, h=h: e.matmul(pb[3][:m, h * 128:h * 128 + m], lhsT=kT[:, h, cols], rhs=qT[:, h, cols], start=True, stop=True),
                   [("kT", s), ("qT", s)], [R3])
            for h in range(H):
                op("pe", lambda e, h=h: e.matmul(pb[5][:, h * 128:(h + 1) * 128], lhsT=kte[:m, s, h * 128:(h + 1) * 128], rhs=v_sb[:m, s, h * 128:(h + 1) * 128],
                                                 start=True, stop=True), [("kte", s), ("v_sb", s)], [R5])
            scv = pb[3][:m, :].rearrange("p (h c) -> p h c", h=H)[:, :, :m]
            op("dve", lambda e: e.tensor_tensor(out=Pm[:m, :, :m], in0=scv, in1=(mask2s if t == 0 else mask2)[:m, :, :m], op=ALU.mult), [R3, ("mask2",)], [("Pm",)])
            for h in range(H):
                gval = math.exp(LG[h] * (16 if t == 0 else 128))
                op("dve", lambda e, h=h, gval=gval: e.scalar_tensor_tensor(out=S[:, h, :], in0=S[:, h, :], scalar=gval, in1=pb[5][:, h * 128:(h + 1) * 128],
                                                                           op0=ALU.mult, op1=ALU.add), [("S", h), R5], [("S", h)])
            op("dve", lambda e: e.tensor_copy(out=S_bf[:, sb_nxt], in_=S), [("S", h) for h in range(H)], [("S_bf", sb_nxt)])

        def part_b():
            for h in range(H):
                op("pe", lambda e, h=h: e.matmul(pb[4][:m, h * 128:(h + 1) * 128], lhsT=Pm[:m, h, :m], rhs=v_sb[:m, s, h * 128:(h + 1) * 128],
                                                 start=True, stop=False), [("Pm",), ("v_sb", s)], [R4])
                op("pe", lambda e, h=h: e.matmul(pb[4][:m, h * 128:(h + 1) * 128], lhsT=qT[:, h, cols], rhs=S_bf[:, sb_cur, h, :],
                                                 start=False, stop=True), [("qT", s), ("S_bf", sb_cur)], [R4])
            for h in range(H):
                op("act", lambda e, h=h: e.activation(out=ret_b[:m, rbi, h * 128:(h + 1) * 128], in_=pb[4][:m, h * 128:(h + 1) * 128], func=AF.Square,
                                                      accum_out=ssh[:, h:h + 1]), [R4, RS], [RS, ("ret_b", rbi, h)])
            op("dve", lambda e: e.scalar_tensor_tensor(out=ah, in0=ssh, scalar=1.0 / 128.0, in1=epsfs[:m, tix, :], op0=ALU.mult, op1=ALU.add),
               [RS, ("epsfs",)], [RS])
            op("pool", lambda e: e.tensor_tensor(out=rh, in0=ah, in1=negh[:m, 0:4], op=ALU.pow), [RS, ("negh",)], [RS])

        def part_c():
            for h in range(H):
                op("dve", lambda e, h=h: e.scalar_tensor_tensor(out=ret_b[:m, rbi, h * 128:(h + 1) * 128], in0=pb[4][:m, h * 128:(h + 1) * 128],
                                                                scalar=rh[:, h:h + 1], in1=gs[:m, s, h * 128:(h + 1) * 128], op0=ALU.mult, op1=ALU.mult),
                   [R4, RS, ("gs", s)], [("ret_b", rbi, h)])

            def tr():
                RT = ("pT",)
                for h in range(H):
                    op("pe", lambda e, h=h: e.transpose(out=pT[:, h * 128: h * 128 + m], in_=ret_b[:m, rbi, h * 128:(h + 1) * 128],
                                                        identity=ident[:m, :m]), [("ret_b", rbi, h), ("ident",)], [RT])
                src = pT[:, 0:512].rearrange("p (k c) -> p k c", c=128)[:, :, :m]
                op("act", lambda e: e.activation(out=mixT[:, 0:4, cols], in_=src, func=AF.Copy), [RT], [("mixTr", s)])
            pending.append(tr)
        return part_a, part_b, part_c

    def post_stages(src_ap, src_reads, s, m, wb_tile, wb_key):
        c = stat_col(3)
        ssa = stat[:m, c:c + 1]; aa = stat[:m, c + 1:c + 2]; ra = stat[:m, c + 2:c + 3]
        RS = ("stat", c)
        xs_ = cur["xs"][s]

        def p0():
            op("act", lambda e: e.activation(out=tmpA_b[:m, 0:1024], in_=src_ap, func=AF.Square, accum_out=ssa), list(src_reads) + [RS], [RS] + RTMP_ALL)

        def p1():
            op("dve", lambda e: e.tensor_scalar(out=aa, in0=ssa, scalar1=1.0 / D, scalar2=EPS, op0=ALU.mult, op1=ALU.add), [RS], [RS])

        def p2():
            op("pool", lambda e: e.tensor_tensor(out=ra, in0=aa, in1=negh[:m, 0:1], op=ALU.pow), [RS, ("negh",)], [RS])

        def p3():
            op("dve", lambda e: e.scalar_tensor_tensor(out=tmpA[:m, :], in0=src_ap, scalar=ra, in1=wb_tile[:m, :], op0=ALU.mult, op1=ALU.mult),
               list(src_reads) + [RS, wb_key], RTMP_ALL)

        def p4():
            op("dve", lambda e: e.tensor_tensor(out=x_sb[:m, xs_, :], in0=x_sb[:m, xs_, :], in1=tmpA[:m, :], op=ALU.add), RTMP_ALL + [("x", xs_)], [("x", xs_)])
        return [p0, p1, p2, p3, p4]

    def post_norm_residual(src_ap, src_reads, s, m, wb_tile, wb_key):
        for f_ in post_stages(src_ap, src_reads, s, m, wb_tile, wb_key):
            f_()

    def load_wug(t, c):
        wslot = pe_ctr["wug"] % NWUG; pe_ctr["wug"] += 1
        wug_slot_of[(t, c)] = wslot
        dma("sp", wug_sb[:, wslot], wug_s[c], [("wug_s", c, 0), ("wug_s", c, 1)], [("wug", wslot)], ("wug", wslot))

    def featmajor_mm(slot_, j, nt):
        bank = pe_ctr["pa"] % 2; pe_ctr["pa"] += 1
        for k in range(8):
            op("pe", lambda e, k=k: e.matmul(pb[bank][:, :nt], lhsT=wslab[:, slot_, k, j * 128:(j + 1) * 128],
                                             rhs=hT[:, k, :nt], start=(k == 0), stop=(k == 7)),
               [("hT", s) for s in range(4)] + [("wslab", slot_)], [("pb", bank)])
        return bank

    def conv_ops(j, nt):
        yi = j % 2
        zb, zk = cur["z"], cur["zk"]
        t1 = tmpA[:, 0:nt]
        t0 = tmpA[:, 512:512 + nt]
        R1 = [("tmpA", 0, i) for i in range(3)]
        R0 = [("tmpA", 1, i) for i in range(3)]
        op("act", lambda e: e.activation(out=yc[:, yi, :nt], in_=zb[:, j, 2:2 + nt], func=AF.Identity, scale=cw[:, j, 2:3]), [(zk, j), ("cw",)], [("yc", yi)])
        op("act", lambda e: e.activation(out=t1, in_=zb[:, j, 1:1 + nt], func=AF.Identity, scale=cw[:, j, 1:2]), [(zk, j), ("cw",)], R1)
        op("act", lambda e: e.activation(out=t0, in_=zb[:, j, 0:nt], func=AF.Identity, scale=cw[:, j, 0:1]), [(zk, j), ("cw",)], R0)
        op("pool", lambda e: e.tensor_tensor(out=yc[:, yi, :nt], in0=yc[:, yi, :nt], in1=t1, op=ALU.add), R1 + [("yc", yi)], [("yc", yi)])
        op("pool", lambda e: e.tensor_tensor(out=yc[:, yi, :nt], in0=yc[:, yi, :nt], in1=t0, op=ALU.add), R0 + [("yc", yi)], [("yc", yi)])
        op("pool", lambda e: e.tensor_copy(out=zb[:, j, 0:2], in_=zb[:, j, nt:nt + 2]), [(zk, j)], [(zk, j)])

    def b_chunk(slot_b, j, nt, mid=None):
        yi = j % 2
        bank = featmajor_mm(slot_b, j, nt)
        if mid is not None:
            mid()
        op("dve", lambda e: e.tensor_tensor(out=mixT[:, 4 + j, :nt], in0=pb[bank][:, :nt], in1=yc[:, yi, :nt], op=ALU.mult),
           [("pb", bank), ("yc", yi)], [("mixT", 4 + j)])

    tile_order = list(range(1, 1 + NT_PROMPT)) + [0]
    prefetched = {}

    def tile_params(t):
        i = tile_order.index(t)
        if t == 0:
            nt, ns, msz, tabidx = DEC, 1, [DEC], [0]
            xsrc = [xs]
            ydst = [ys]
        else:
            nt, ns, msz = TT, 4, [128] * 4
            base = (t - 1) * TT
            tabidx = [1 + (t - 1) * 4 + s for s in range(4)]
            xsrc = [xp[base + s * 128: base + (s + 1) * 128, :] for s in range(4)]
            ydst = [yp[base + s * 128: base + (s + 1) * 128, :] for s in range(4)]
        slots = [(4 * i + s) % 5 for s in range(ns)]
        return dict(t=t, nt=nt, ns=ns, msz=msz, tabidx=tabidx, xsrc=xsrc, ydst=ydst, slots=slots)

    def load_x(tp, s):
        sl = tp["slots"][s]
        dma("sp", x_sb[:tp["msz"][s], sl, :], tp["xsrc"][s], [], [("x", sl)], ("x", sl))
        prefetched[("x", tp["t"], s)] = True

    n1_recs = {}

    def n1_rec(t, s, tp):
        if (t, s) not in n1_recs:
            n1_recs[(t, s)] = {"st": norm_stages(t, s, tp["msz"][s], wb_pre_mix, ("wb_pre_mix",), xslot=tp["slots"][s], sq_qrot=(s >= 2)), "done": 0}
        return n1_recs[(t, s)]

    def adv(rec, upto):
        while rec["done"] < upto:
            rec["st"][rec["done"]]()
            rec["done"] += 1

    def run_tile(t):
        tp = tile_params(t)
        nt, ns, msz, tabidx, ydst = tp["nt"], tp["ns"], tp["msz"], tp["tabidx"], tp["ydst"]
        i = tile_order.index(t)
        tp_next = tile_params(tile_order[i + 1]) if i + 1 < len(tile_order) else None
        if t == 0:
            cur.update(z=z_s, zk="z_s", hf=halo_s, hfk="halo_s")
        else:
            cur.update(z=z, zk="z", hf=halo_f, hfk="halo_f")
        cur["xs"] = tp["slots"]
        for s in range(ns):
            if not prefetched.get(("x", t, s)):
                load_x(tp, s)
        recs = [n1_rec(t, s, tp) for s in range(ns)]
        for p0 in range(0, ns, 2):
            grp = list(range(p0, min(p0 + 2, ns)))
            for stage in range(1, 4):
                for s in grp:
                    adv(recs[s], stage)
            if p0 == 0:
                issue_casts(8)
        slot = 0 if prefetched.get(("slab", t)) else load_slab_into(0, 0)
        for s in range(min(2, ns)):
            adv(recs[s], 4)
        first_tile = not tables_built[0]
        if first_tile:
            tables_built[0] = True
            build_tables()
        n1_hooks = {}
        for s in range(ns):
            def hook(s=s):
                adv(recs[s], 5)
                if s + 2 < ns:
                    adv(recs[s + 2], 4)
            n1_hooks[s] = hook
        order = [0, 1, 2, 3]
        for gi, g in enumerate(order):
            nxt = order[gi + 1] if gi + 1 < 4 else 6
            if gi == 0 and prefetched.get(("slab1", t)):
                nslot = 1 - slot
            else:
                nslot = load_slab_into(nxt, 1 - slot)
            proj_tokmajor(t, g, slot, ns, msz, tabidx, n1_hooks if gi == 0 else None)
            slot = nslot
            issue_casts(6)
        flush_pending()
        slot_hc = slot
        slot_c = load_slab_into(5, 1 - slot_hc)
        dma("sp", wout_sb, wout_s, [("wout_s",)], [("facc", s, hh) for s in range(4) for hh in range(2)], ("wout",))
        for j in range(4):
            bank = featmajor_mm(slot_hc, j, nt)
            hi = j % 2
            op("act", lambda e, bank=bank, hi=hi: e.activation(out=hc_sb[:, hi, :nt], in_=pb[bank][:, :nt], func=AF.Copy), [("pb", bank)], [("hc_sb", hi)])
            bank2 = featmajor_mm(slot_c, j, nt)
            op("dve", lambda e, bank2=bank2, hi=hi, j=j, zb=cur["z"]: e.tensor_tensor(out=zb[:, j, 2:2 + nt], in0=pb[bank2][:, :nt], in1=hc_sb[:, hi, :nt], op=ALU.mult),
               [("pb", bank2), ("hc_sb", hi)], [(cur["zk"], j)])
        slot_b = load_slab_into(4, slot_hc)
        issue_casts(6)
        load_wug(t, 0)
        load_wug(t, 1)
        dma("sp", wdn_sb, wdn_s[0], [("wdn_s", 0)], [("wdn_sb",)], ("wdn",))
        parts = [retention_parts(t, s, msz[s]) for s in range(ns)]
        parts[0][0]()
        parts[0][1]()
        if ns == 4:
            for s in range(ns):
                if s + 1 < ns:
                    parts[s + 1][0]()
                conv_ops(s, nt)
                b_chunk(slot_b, s, nt, mid=parts[s][2])
                flush_pending(keep=1)
                if s + 1 < ns:
                    parts[s + 1][1]()
            flush_pending()
        else:
            parts[0][2]()
            for j in range(4):
                conv_ops(j, nt)
                b_chunk(slot_b, j, nt)
            flush_pending()
        def wout_mm(s):
            m = msz[s]
            cols = slice(s * 128, s * 128 + m)
            if s % 2 == 0:
                pbuf, regs = p67, [("p67", 0), ("p67", 1)]
            else:
                pbuf, regs = p34, [("pb", 3), ("pb", 4)]
            for half in range(2):
                for k in range(8):
                    rk = ("mixTr", s) if k < 4 else ("mixT", k)
                    op("pe", lambda e, k=k, half=half: e.matmul(pbuf[:m, half * 512:(half + 1) * 512], lhsT=mixT[:, k, cols],
                                                                rhs=wout_sb[:, k, half * 512:(half + 1) * 512], start=(k == 0), stop=(k == 7)),
                       [rk] + [("facc", q, hh) for q in range(4) for hh in range(2)], [regs[half]])
            return post_stages(pbuf[:m, :], regs, s, m, wb_post_mix, ("wb_post_mix",))
        n2b = {}
        for s in range(ns + 2):
            post = wout_mm(s) if s < ns else []
            n2 = norm_stages(t, s - 1, msz[s - 1], wb_pre_ffn, ("wb_pre_ffn",)) if 1 <= s <= ns else []
            for k in range(5):
                if k < len(post):
                    post[k]()
                if n2 and k < 4:
                    n2[k]()
            if n2:
                n2b[s - 1] = n2[4]
            if 2 <= s:
                n2b[s - 2]()
        issue_casts(6)
        if tp_next is not None:
            load_x(tp_next, 0)
            load_slab_into(0, 0)
            prefetched[("slab", tile_order[i + 1])] = True
            load_slab_into(1, 1)
            prefetched[("slab1", tile_order[i + 1])] = True
        def emit_down(g):
                if g == 3 and tp_next is not None:
                    adv(n1_rec(tile_order[i + 1], 0, tp_next), 5)
                for s in range(ns):
                    m = msz[s]
                    cols = slice(s * 128, s * 128 + m)
                    for half in range(2):
                        di = pe_ctr["dn"] % 4; pe_ctr["dn"] += 1
                        RH, dbank = [(("p67", 0), p67[:, 0:512]), (("p67", 1), p67[:, 512:1024]), (("pb", 3), pb[3]), (("pb", 4), pb[4])][di]
                        hs = slice(half * 512, (half + 1) * 512)
                        for cc in range(8):
                            op("pe", lambda e, cc=cc, hs=hs, m=m, cols=cols, dbank=dbank: e.matmul(dbank[:m, :], lhsT=act[:, cc, cols], rhs=wdn_sb[:, cc, hs],
                                                                                                     start=(cc == 0), stop=(cc == 7)),
                               [("act", cc), ("wdn_sb",)], [RH])
                        if g == 0:
                            op("act", lambda e, s=s, m=m, hs=hs, dbank=dbank: e.activation(out=f_acc[:m, s, hs], in_=dbank[:m, :], func=AF.Copy), [RH], [("facc", s, half)])
                        else:
                            op("dve", lambda e, s=s, m=m, hs=hs, dbank=dbank: e.tensor_tensor(out=f_acc[:m, s, hs], in0=f_acc[:m, s, hs], in1=dbank[:m, :], op=ALU.add),
                               [RH, ("facc", s, half)], [("facc", s, half)])
                    if g == 3:
                        post = post_stages(f_acc[:m, s, :], [("facc", s, 0), ("facc", s, 1)], s, m, wb_post_ffn, ("wb_post_ffn",))
                        nrec = None
                        if tp_next is not None:
                            tn = tile_order[i + 1]
                            if 1 <= s - 2 < tp_next["ns"]:
                                adv(n1_rec(tn, s - 2, tp_next), 5)
                            if 1 <= s - 1 < tp_next["ns"]:
                                nrec = n1_rec(tn, s - 1, tp_next)
                        for k in range(5):
                            post[k]()
                            if nrec is not None and k < 4:
                                adv(nrec, k + 1)
                        dma("sp", ydst[s], x_sb[:m, tp["slots"][s], :], [("x", tp["slots"][s])], [], ("y", tp["slots"][s]), is_out=True)
                        if tp_next is not None and s + 1 < tp_next["ns"]:
                            load_x(tp_next, s + 1)

        for g in range(4):
            if g == 3 and tp_next is not None:
                adv(n1_rec(tile_order[i + 1], 0, tp_next), 4)
            for cc in range(8):
                c = g * 8 + cc
                if g > 0 and cc == 0 and t == 0:
                    emit_down(g - 1)
                wslot = wug_slot_of[(t, c)]
                if c + 2 < 32:
                    load_wug(t, c + 2)
                if g > 0 and cc == 1:
                    dma("sp", wdn_sb, wdn_s[g], [("wdn_s", g)], [("wdn_sb",)], ("wdn",))
                issue_casts(2 if cc else 3)
                if t == 0:
                    b3 = c % 3
                    PU, RU, PG, RG = [(pb[0], ("pb", 0), pb[1], ("pb", 1)), (pb[3], ("pb", 3), pb[4], ("pb", 4)),
                                      (pb[5], ("pb", 5), p67[:, 0:512], ("p67", 0))][b3]
                    if b3 < 2:
                        upv, RUP = up_sb[:, b3, :], ("upsb", b3)
                        accv, RACC = acc[:, b3, :], ("acc", b3)
                        glv, RGL = gl[:, b3, :], ("gl", b3)
                    else:
                        upv, RUP = hc_sb[:, 0, :], ("hc_sb", 0)
                        accv, RACC = hc_sb[:, 1, :], ("hc_sb", 1)
                        glv, RGL = yc.bitcast(BF16)[:, 0, :], ("yc", 0)
                else:
                    b = c % 2
                    PU, RU, PG, RG = [(pb[0], ("pb", 0), pb[1], ("pb", 1)), (pb[3], ("pb", 3), pb[4], ("pb", 4))][b]
                    upv, RUP = up_sb[:, b, :], ("upsb", b)
                    accv, RACC = acc[:, b, :], ("acc", b)
                    glv, RGL = gl[:, b, :], ("gl", b)
                hf, hfk = cur["hf"], cur["hfk"]
                for k in range(8):
                    op("pe", lambda e, k=k, wslot=wslot, PU=PU: e.matmul(PU[:, :nt], lhsT=wug_sb[:, wslot, 0, k, :], rhs=hT[:, k, :nt],
                                                                          start=(k == 0), stop=(k == 7)),
                       [("hT", s) for s in range(4)] + [("wug", wslot)], [RU])
                for k in range(8):
                    op("pe", lambda e, k=k, wslot=wslot, PG=PG: e.matmul(PG[:, :nt], lhsT=wug_sb[:, wslot, 1, k, :], rhs=hT[:, k, :nt],
                                                                          start=(k == 0), stop=(k == 7)),
                       [("hT", s) for s in range(4)] + [("wug", wslot)], [RG])
                if g > 0 and cc == 0 and t != 0:
                    emit_down(g - 1)
                op("pool", lambda e, upv=upv, c=c, hf=hf: e.tensor_copy(out=upv[:, 0:2], in_=hf[:, c, :]), [(hfk,)], [RUP])
                op("act", lambda e, upv=upv, PU=PU: e.activation(out=upv[:, 2:2 + nt], in_=PU[:, :nt], func=AF.Copy), [RU, RUP], [RUP])
                op("pool", lambda e, upv=upv, c=c, hf=hf: e.tensor_copy(out=hf[:, c, :], in_=upv[:, nt:nt + 2]), [RUP, (hfk,)], [(hfk,)])
                op("act", lambda e, accv=accv, PU=PU, c=c: e.activation(out=accv[:, :nt], in_=PU[:, :nt], func=AF.Identity, scale=fw[:, c, 2:3]),
                   [RU, ("fw",)], [RACC])
                op("dve", lambda e, accv=accv, upv=upv, c=c: e.scalar_tensor_tensor(out=accv[:, :nt], in0=upv[:, 1:1 + nt], scalar=fw[:, c, 1:2], in1=accv[:, :nt],
                                                                                    op0=ALU.mult, op1=ALU.add), [RUP, RACC, ("fw",)], [RACC])
                op("dve", lambda e, accv=accv, upv=upv, c=c: e.scalar_tensor_tensor(out=accv[:, :nt], in0=upv[:, 0:nt], scalar=fw[:, c, 0:1], in1=accv[:, :nt],
                                                                                    op0=ALU.mult, op1=ALU.add), [RUP, RACC, ("fw",)], [RACC])
                op("act", lambda e, accv=accv, glv=glv: e.activation(out=glv[:, :nt], in_=accv[:, :nt], func=AF.Gelu_apprx_tanh), [RACC], [RGL])
                op("dve", lambda e, glv=glv, PG=PG, cc=cc: e.tensor_tensor(out=act[:, cc, :nt], in0=PG[:, :nt], in1=glv[:, :nt], op=ALU.mult),
                   [RG, RGL], [("act", cc)])
        emit_down(3)

    def emit_cache_outputs(cc_o, cf_o, zb, zk, hf, hfk, tag):
        R5 = ("pb", 5)
        for r in range(2):
            op("pe", lambda e, r=r: e.transpose(out=pb[5][:32, r * 128:(r + 1) * 128], in_=hf[:, :, r], identity=ident_f),
               [(hfk,), ("ident_f",)], [R5])
            op("pe", lambda e, r=r: e.transpose(out=pb[5][:4, (2 + r) * 128:(3 + r) * 128], in_=zb[:, :, r], identity=ident_f),
               [(zk, j) for j in range(4)] + [("ident_f",)], [R5])
        op("act", lambda e: e.activation(out=tmpA[:32, 0:512], in_=pb[5][:32, :], func=AF.Copy), [R5], RTMP_ALL)
        for r in range(2):
            dma("sp", cf_o[r].rearrange("(c p) -> c p", p=128), tmpA[:32, r * 128:(r + 1) * 128], RTMP_ALL, [], ("so", tag), is_out=True)
            dma("sp", cc_o[r].rearrange("(j p) -> j p", p=128), tmpA[:4, (2 + r) * 128:(3 + r) * 128], RTMP_ALL, [], ("so", tag), is_out=True)

    SALL = [("S", h) for h in range(H)]
    op("dve", lambda e: e.memset(S, 0.0), [], SALL)
    op("dve", lambda e: e.memset(S_bf, 0.0), [], [("S_bf", 0), ("S_bf", 1)])
    op("dve", lambda e: e.memset(z[:, :, 0:2], 0.0), [], [("z", j) for j in range(4)])
    op("dve", lambda e: e.memset(halo_f, 0.0), [], [("halo_f",)])
    load_slow_consts()
    for t in tile_order[:-1]:
        run_tile(t)
    dma("sp", srp.rearrange("h d e -> d h e"), S, SALL, [], ("so", "p0"), is_out=True)
    dma("sp", S, st_in.rearrange("h d e -> d h e"), [], SALL, ("cst2",))
    sbn = pe_ctr["sbf"] % 2
    op("dve", lambda e: e.tensor_copy(out=S_bf[:, sbn], in_=S), SALL, [("S_bf", sbn)])
    run_tile(0)
    dma("sp", srs.rearrange("h d e -> d h e"), S, SALL, [], ("so", "s0"), is_out=True)
    emit_cache_outputs(ccp, cfp, z, "z", halo_f, "halo_f", "p")
    emit_cache_outputs(ccs, cfs, z_s, "z_s", halo_s, "halo_s", "s")
    P.emit()
    return nc


_CACHE = {}


def kernel(x_prompt, x_sample, state_ret, cache_conv, cache_ffn_conv,
           w_in, w_out, conv_w, ret_norm_w, pre_mix_w, post_mix_w,
           pre_ffn_w, post_ffn_w, w_up, w_gate, ffn_conv_w, w_down):
    f = lambda a: np.ascontiguousarray(np.asarray(a, dtype=np.float32))
    x_prompt = f(x_prompt); x_sample = f(x_sample); state_ret = f(state_ret)
    cache_conv = f(cache_conv); cache_ffn_conv = f(cache_ffn_conv)
    shared = {
        "w_in": f(w_in)[0], "w_out": f(w_out)[0], "conv_w": f(conv_w)[0], "ret_w": f(ret_norm_w),
        "pre_mix": f(pre_mix_w), "post_mix": f(post_mix_w), "pre_ffn": f(pre_ffn_w), "post_ffn": f(post_ffn_w),
        "w_up": f(w_up)[0], "w_gate": f(w_gate)[0], "ffn_cw": f(ffn_conv_w)[0], "w_down": f(w_down)[0],
    }
    if "nc" not in _CACHE:
        _CACHE["nc"] = build_program()
    nc = _CACHE["nc"]
    in_maps = []
    for c in range(NCORES):
        m = dict(shared)
        m["xp"] = x_prompt[c]; m["xs"] = x_sample[c]
        m["st"] = state_ret[0, c]; m["cc"] = cache_conv[0, c]; m["cf"] = cache_ffn_conv[0, c]
        in_maps.append(m)
    res = run_bass_kernel_spmd(nc, in_maps, core_ids=list(range(NCORES)))
    r = res.results
    st = lambda k: np.stack([np.asarray(r[c][k], dtype=np.float32) for c in range(NCORES)])
    yp = st("yp"); ys = st("ys")
    return (yp, ys, st("srp")[None], st("ccp")[None], st("cfp")[None],
            st("srs")[None], st("ccs")[None], st("cfs")[None])
```

```python
import math
import numpy as np
import concourse.bass as bass
import concourse.mybir as mybir
from concourse.bass_utils import run_bass_kernel_spmd

F32 = mybir.dt.float32
BF16 = mybir.dt.bfloat16
I32 = mybir.dt.int32
AF = mybir.ActivationFunctionType
ALU = mybir.AluOpType

D = 1024
SEQ = 4096
DEC = 16
PAST_LEN = 4096
TT = 512
NT_PROMPT = SEQ // TT
H = 4
DFF = 4096
EPS = 1e-6
NCORES = 8
LG = [math.log(1.0 - 2.0 ** (-5.0 - h)) for h in range(H)]
SC = 128.0 ** -0.5
TWO_PI = 2.0 * math.pi


GROUP_KEYS = ("castug", "cst", "cstslow", "cst2", "so")


class Prog:
    def __init__(self, nc):
        self.nc = nc
        self.ops = []
        self.last_w = {}
        self.readers = {}
        self.out_dma = []

    def add(self, eng, fn, reads=(), writes=(), dma_key=None, is_out=False):
        idx = len(self.ops)
        deps = set()
        for r in reads:
            if r in self.last_w:
                deps.add(self.last_w[r])
        for w in writes:
            if w in self.last_w:
                deps.add(self.last_w[w])
            for rd in self.readers.get(w, {}).values():
                deps.add(rd)
        for w in writes:
            self.last_w[w] = idx
            self.readers[w] = {}
        for r in reads:
            if r not in writes:
                rk = eng if dma_key is None else ("dma", idx)
                self.readers.setdefault(r, {})[rk] = idx
        deps.discard(idx)
        if dma_key is not None and dma_key[0] in GROUP_KEYS:
            deps = {d for d in deps if self.ops[d]["dma_key"] != dma_key}
        self.ops.append(dict(eng=eng, fn=fn, deps=deps, dma_key=dma_key, idx=idx))
        if is_out:
            self.out_dma.append(idx)
        return idx

    def emit(self):
        nc = self.nc
        ops = self.ops
        needed = set()
        for o in ops:
            for d in o["deps"]:
                p = ops[d]
                if p["eng"] == "pe" and o["eng"] == "pe" and p["dma_key"] is None and o["dma_key"] is None:
                    continue
                needed.add(d)
        eng_names = ["pe", "act", "dve", "pool", "sp"]
        counts = {e: 0 for e in eng_names}
        dma_counts = {}
        dma_keys = []
        for o in ops:
            if o["dma_key"] is not None:
                k = o["dma_key"]
                if k not in dma_counts:
                    dma_counts[k] = 0
                    dma_keys.append(k)
                dma_counts[k] += 16
                o["tok"] = (("dma", k), dma_counts[k])
                o["is_dma"] = True
            elif o["idx"] in needed:
                counts[o["eng"]] += 1
                o["tok"] = (("eng", o["eng"]), counts[o["eng"]])
            else:
                o["tok"] = None
        for o in ops:
            if o["dma_key"] is not None and o["dma_key"][0] in GROUP_KEYS:
                o["tok"] = (("dma", o["dma_key"]), dma_counts[o["dma_key"]])
        import contextlib
        with contextlib.ExitStack() as es:
            sems = {}
            for e in eng_names:
                sems[("eng", e)] = es.enter_context(nc.semaphore("s_" + e))
            for i, k in enumerate(dma_keys):
                sems[("dma", k)] = es.enter_context(nc.semaphore("d%d" % i))
            block = es.enter_context(nc.Block())
            per_eng = {e: [o for o in ops if o["eng"] == e] for e in eng_names}
            final_waits = []
            for k in dma_keys:
                final_waits.append((("dma", k), dma_counts[k]))

            def run(engname, eng, is_last=False):
                waited = {}
                for o in per_eng[engname]:
                    need = {}
                    for d in o["deps"]:
                        p = ops[d]
                        if p["tok"] is None:
                            continue
                        if p["eng"] == "pe" and engname == "pe" and p["dma_key"] is None and o["dma_key"] is None:
                            continue
                        sk, val = p["tok"]
                        if need.get(sk, 0) < val:
                            need[sk] = val
                    todo = [(sk, val) for sk, val in need.items() if waited.get(sk, 0) < val]
                    for sk, val in todo[:-1]:
                        eng.wait_ge(sems[sk], val)
                        waited[sk] = val
                    inst = o["fn"](eng)
                    if todo:
                        sk, val = todo[-1]
                        inst._wait_ge(sems[sk], val)
                        waited[sk] = val
                    if o["tok"] is not None:
                        sk, val = o["tok"]
                        inst.then_inc(sems[sk], 16 if sk[0] == "dma" else 1)
                if is_last:
                    for sk, val in final_waits:
                        eng.wait_ge(sems[sk], val)

            @block.tensor
            def _(e):
                run("pe", e)

            @block.scalar
            def _(e):
                run("act", e)

            @block.vector
            def _(e):
                run("dve", e)

            @block.gpsimd
            def _(e):
                run("pool", e)

            @block.sync
            def _(e):
                run("sp", e, is_last=True)


def build_program():
    nc = bass.Bass("TRN2", target_bir_lowering=False)
    P = Prog(nc)

    def din(name, shape):
        return nc.dram_tensor(name, shape, F32, kind="ExternalInput").ap()

    def dout(name, shape):
        return nc.dram_tensor(name, shape, F32, kind="ExternalOutput").ap()

    xp = din("xp", [SEQ, D]); xs = din("xs", [DEC, D])
    st_in = din("st", [H, 128, 128]); cc_in = din("cc", [2, 512]); cf_in = din("cf", [2, DFF])
    w_in = din("w_in", [D, 3584]); w_out = din("w_out", [D, D])
    conv_w = din("conv_w", [3, 512]); ret_w = din("ret_w", [1, 512])
    pre_mix = din("pre_mix", [1, D]); post_mix = din("post_mix", [1, D])
    pre_ffn = din("pre_ffn", [1, D]); post_ffn = din("post_ffn", [1, D])
    w_up = din("w_up", [D, DFF]); w_gate = din("w_gate", [D, DFF])
    ffn_cw = din("ffn_cw", [3, DFF]); w_down = din("w_down", [DFF, D])
    yp = dout("yp", [SEQ, D]); ys = dout("ys", [DEC, D])
    srp = dout("srp", [H, 128, 128]); ccp = dout("ccp", [2, 512]); cfp = dout("cfp", [2, DFF])
    srs = dout("srs", [H, 128, 128]); ccs = dout("ccs", [2, 512]); cfs = dout("cfs", [2, DFF])

    win_s = nc.dram_tensor("win_s", [7, 128, 8, 512], BF16).ap()
    wout_s = nc.dram_tensor("wout_s", [128, 8, 1024], BF16).ap()
    wug_s = nc.dram_tensor("wug_s", [32, 128, 2, 8, 128], BF16).ap()
    wdn_s = nc.dram_tensor("wdn_s", [4, 128, 8, 1024], BF16).ap()

    def sb(name, shape, dt):
        return nc.alloc_sbuf_tensor(name, shape, dt).ap()

    x_sb = sb("x_sb", [128, 5, D], F32)
    hT = sb("hT", [128, 8, TT], BF16)
    wslab = sb("wslab", [128, 2, 8, 512], BF16)
    f_acc = sb("f_acc", [128, 4, D], F32)
    wout_sb = f_acc.bitcast(BF16).rearrange("p a (b c) -> p (a b) c", c=1024)
    NWUG = 3
    wug_sb = sb("wug_sb", [128, NWUG, 2, 8, 128], BF16)
    qT = sb("qT", [128, H, TT], BF16); kT = sb("kT", [128, H, TT], BF16)
    kte = sb("kte", [128, 4, 512], BF16); v_sb = sb("v_sb", [128, 4, 512], BF16); gs = sb("gs", [128, 4, 512], BF16)
    qrot = sb("qrot", [128, 2, 512], BF16)
    tmpA = sb("tmpA", [128, 1024], F32)
    z = sb("z", [128, 4, TT + 2], F32)
    hc_sb = sb("hc_sb", [128, 2, TT], F32); yc = sb("yc", [128, 2, TT], F32)
    mixT = sb("mixT", [128, 8, TT], BF16)
    up_sb = sb("up_sb", [128, 2, TT + 2], F32); acc = sb("acc", [128, 2, TT], F32); gl = sb("gl", [128, 2, TT], BF16)
    act = sb("act", [128, 8, TT], BF16)
    wdn_sb = sb("wdn_sb", [128, 8, 1024], BF16)
    wb_pre_mix = sb("wb_pre_mix", [128, D], F32); wb_post_mix = sb("wb_post_mix", [128, D], F32)
    wb_pre_ffn = sb("wb_pre_ffn", [128, D], F32); wb_post_ffn = sb("wb_post_ffn", [128, D], F32)
    wb_ret = sb("wb_ret", [128, 512], F32)
    cosT = sb("cosT", [128, 33, 64], F32); sinT = sb("sinT", [128, 33, 64], F32)
    mask2 = sb("mask2", [128, H, 128], F32)
    mask2s = sb("mask2s", [128, H, DEC], F32)
    ident_f = sb("ident_f", [128, 128], F32)
    te_t = sb("te_t", [128, 2, H], F32); epsfs = sb("epsfs", [128, 2, H], F32)
    S = sb("S", [128, H, 128], F32); S_bf = sb("S_bf", [128, 2, H, 128], BF16)
    Pm = sb("Pm", [128, H, 128], BF16)
    ret_b = sb("ret_b", [128, 2, 512], BF16)
    hb = sb("hb", [128, 2, D], BF16)
    halo_f = sb("halo_f", [128, 32, 2], F32)
    z_s = sb("z_s", [128, 4, DEC + 2], F32)
    halo_s = sb("halo_s", [128, 32, 2], F32)
    cur = {"z": z, "zk": "z", "hf": halo_f, "hfk": "halo_f", "xs": [0, 1, 2, 3]}
    cw = sb("cw", [128, 4, 3], F32); fw = sb("fw", [128, 32, 3], F32)
    ident = sb("ident", [128, 128], BF16)
    negh = sb("negh", [128, 8], F32)
    stat = sb("stat", [128, 96], F32)
    inv_f = sb("inv_f", [128, 64], F32)
    ptmp = wdn_sb.bitcast(F32).rearrange("p a c -> p (a c)")

    def ps(name, shape, dt):
        return nc.alloc_psum_tensor(name, shape, dt).ap()

    p34 = ps("p34", [128, 1024], F32)
    pb = [ps("pb0", [128, 512], F32), ps("pb1", [128, 512], F32), None,
          p34[:, 0:512], p34[:, 512:1024], ps("pb5", [128, 512], F32)]
    pT = ps("pT", [128, 1024], BF16)
    p67 = ps("p67", [128, 1024], F32)

    def dma(q, out, in_, reads, writes, key, is_out=False, slow=False):
        def fn(e, out=out, in_=in_):
            if slow:
                return e.dma_start(out=out, in_=in_, allow_slow_non_contiguous=True)
            return e.dma_start(out=out, in_=in_)
        P.add(q, fn, reads, writes, dma_key=key, is_out=is_out)

    def op(eng, f, reads, writes):
        reads = list(reads)
        if any(k[0] == "pt" for k in list(reads) + list(writes)):
            reads.append(("wdn_sb",))
        P.add(eng, f, reads, writes)

    stat_ctr = [0]

    def stat_col(n=1):
        assert n <= 12
        k = stat_ctr[0] % 8
        stat_ctr[0] += 1
        return 12 * k

    w_in_v = w_in.rearrange("(k p) (g c) -> g p k c", p=128, c=512)
    w_up_v = w_up.rearrange("(k p) (c j) -> c p k j", p=128, j=128)
    w_gate_v = w_gate.rearrange("(k p) (c j) -> c p k j", p=128, j=128)
    w_dn_v = w_down.rearrange("(g cc p) n -> g p cc n", cc=8, p=128)
    cast_queue = []
    for g in range(7):
        cast_queue.append(lambda g=g: dma("pool", win_s[g], w_in_v[g], [], [("win_s", g)], ("castx", "in", g)))
    cast_queue.append(lambda: dma("pool", wout_s, w_out.rearrange("(k p) c -> p k c", p=128), [], [("wout_s",)], ("castx", "out")))
    for g in range(4):
        for c2 in range(8):
            c0 = g * 8 + c2
            cast_queue.append(lambda c0=c0, g=g: dma("pool", wug_s[c0, :, 0], w_up_v[c0], [], [("wug_s", c0, 0)], ("castug", c0)))
            cast_queue.append(lambda c0=c0, g=g: dma("pool", wug_s[c0, :, 1], w_gate_v[c0], [], [("wug_s", c0, 1)], ("castug", c0)))
        cast_queue.append(lambda g=g: dma("pool", wdn_s[g], w_dn_v[g], [], [("wdn_s", g)], ("castx", "d", g)))

    def issue_casts(n):
        for _ in range(n):
            if cast_queue:
                cast_queue.pop(0)()

    dma("sp", wb_pre_mix, pre_mix.partition_broadcast(128), [], [("wb_pre_mix",)], ("cst",))
    dma("sp", wb_post_mix, post_mix.partition_broadcast(128), [], [("wb_post_mix",)], ("cst",))
    dma("sp", wb_pre_ffn, pre_ffn.partition_broadcast(128), [], [("wb_pre_ffn",)], ("cst",))
    dma("sp", wb_post_ffn, post_ffn.partition_broadcast(128), [], [("wb_post_ffn",)], ("cst",))
    dma("sp", wb_ret, ret_w.partition_broadcast(128), [], [("wb_ret",)], ("cst",))
    for i in range(3):
        dma("sp", cw[:, :, i], conv_w[i].rearrange("(j p) -> p j", p=128), [], [("cw",)], ("cstslow",), slow=True)
    for i in range(3):
        dma("sp", fw[:, :, i], ffn_cw[i].rearrange("(c p) -> p c", p=128), [], [("fw",)], ("cstslow",), slow=True)

    op("dve", lambda e: e.memset(negh, -0.5), [], [("negh",)])
    idi = ptmp[:, 0:128].bitcast(I32)
    idf = ptmp[:, 128:256]
    op("pool", lambda e: e.iota(idi, pattern=[[1, 128]], base=0, channel_multiplier=-1), [], [("pt", 0)])
    op("dve", lambda e: e.tensor_copy(out=idf, in_=idi), [("pt", 0)], [("pt", 1)])
    op("dve", lambda e: e.tensor_scalar(out=ident, in0=idf, scalar1=0.0, scalar2=None, op0=ALU.is_equal), [("pt", 1)], [("ident",)])
    op("dve", lambda e: e.tensor_scalar(out=ident_f, in0=idf, scalar1=0.0, scalar2=None, op0=ALU.is_equal), [("pt", 1)], [("ident_f",)])

    def build_tables():
        dji = ptmp[:, 256:384].bitcast(I32)
        djf = ptmp[:, 384:512]
        ef = ptmp[:, 512:640]
        pidx_i = ptmp[:, 640:641].bitcast(I32)
        pidx_f = ptmp[:, 641:642]
        tv = ptmp[:, 642:643]
        op("pool", lambda e: e.iota(dji, pattern=[[-1, 128]], base=0, channel_multiplier=1), [], [("pt", 2)])
        op("dve", lambda e: e.tensor_copy(out=djf, in_=dji), [("pt", 2)], [("pt", 3)])
        op("dve", lambda e: e.tensor_scalar(out=djf, in0=djf, scalar1=0.0, scalar2=2.0, op0=ALU.max, op1=ALU.mult), [("pt", 3)], [("pt", 3)])
        op("pool", lambda e: e.iota(pidx_i, pattern=[[0, 1]], base=0, channel_multiplier=1), [], [("pt", 4)])
        op("dve", lambda e: e.tensor_copy(out=pidx_f, in_=pidx_i), [("pt", 4)], [("pt", 5)])
        for ti, L in ((0, 16), (1, 128)):
            op("dve", lambda e, L=L: e.tensor_scalar(out=ef, in0=djf, scalar1=float(-L), scalar2=None, op0=ALU.add), [("pt", 3)], [("pt", 6)])
            for h in range(H):
                if ti == 0:
                    op("act", lambda e, h=h: e.activation(out=mask2s[:, h, :], in_=ef[:, 0:DEC], func=AF.Exp, scale=LG[h]), [("pt", 6)], [("mask2",)])
                else:
                    op("act", lambda e, h=h: e.activation(out=mask2[:, h, :], in_=ef, func=AF.Exp, scale=LG[h]), [("pt", 6)], [("mask2",)])
            op("dve", lambda e, L=L: e.tensor_scalar(out=tv, in0=pidx_f, scalar1=-1.0, scalar2=float(L - 1), op0=ALU.mult, op1=ALU.add), [("pt", 5)], [("pt", 7)])
            for h in range(H):
                op("act", lambda e, ti=ti, h=h: e.activation(out=te_t[:, ti, h:h + 1], in_=tv, func=AF.Exp, scale=LG[h]), [("pt", 7)], [("te_t",)])
            op("dve", lambda e: e.tensor_scalar(out=tv, in0=pidx_f, scalar1=1.0, scalar2=None, op0=ALU.add), [("pt", 5)], [("pt", 7)])
            for h in range(H):
                op("act", lambda e, ti=ti, h=h: e.activation(out=epsfs[:, ti, h:h + 1], in_=tv, func=AF.Exp, scale=-2.0 * LG[h]), [("pt", 7)], [("epsfs",)])
        op("dve", lambda e: e.tensor_scalar(out=te_t, in0=te_t, scalar1=SC, scalar2=None, op0=ALU.mult), [("te_t",)], [("te_t",)])
        op("dve", lambda e: e.tensor_scalar(out=epsfs, in0=epsfs, scalar1=EPS, scalar2=None, op0=ALU.mult), [("epsfs",)], [("epsfs",)])
        op("dve", lambda e: e.memset(mask2[64:128, :, 0:64], 0.0), [("mask2",)], [("mask2",)])

        for i in range(64):
            op("dve", lambda e, i=i: e.memset(inv_f[:, i:i + 1], float(np.float32(10000.0 ** (-(2.0 * i) / 128.0)))), [], [("inv_f",)])
        pos_i = ptmp[:, 700:733].bitcast(I32)
        pos_f = ptmp[:, 740:773]
        op("pool", lambda e: e.iota(pos_i[:, 1:33], pattern=[[128, 32]], base=0, channel_multiplier=1), [], [("pt", 8)])
        op("pool", lambda e: e.iota(pos_i[:, 0:1], pattern=[[0, 1]], base=PAST_LEN, channel_multiplier=1), [("pt", 8)], [("pt", 8)])
        op("dve", lambda e: e.tensor_copy(out=pos_f, in_=pos_i), [("pt", 8)], [("pt", 9)])
        NB = 11
        ang = ptmp[:, 1024:1024 + NB * 64].rearrange("p (n i) -> p n i", i=64)
        tq = ptmp[:, 1792:1792 + NB * 64].rearrange("p (n i) -> p n i", i=64)
        ki = ptmp[:, 2560:2560 + NB * 64].bitcast(I32).rearrange("p (n i) -> p n i", i=64)
        rr = ptmp[:, 3328:3328 + NB * 64].rearrange("p (n i) -> p n i", i=64)
        for blk in range(3):
            n0 = blk * NB
            R_ANG, R_TQ, R_KI, R_RR = ("pt", 10), ("pt", 11), ("pt", 12), ("pt", 13)
            op("dve", lambda e, n0=n0: e.tensor_tensor(out=ang, in0=pos_f[:, n0:n0 + NB].unsqueeze(2).broadcast_to([128, NB, 64]),
                                                       in1=inv_f.unsqueeze(1).broadcast_to([128, NB, 64]), op=ALU.mult),
               [("pt", 9), ("inv_f",)], [R_ANG])
            op("dve", lambda e: e.tensor_scalar(out=tq, in0=ang, scalar1=1.0 / TWO_PI, scalar2=None, op0=ALU.mult), [R_ANG], [R_TQ])
            op("dve", lambda e: e.tensor_copy(out=ki, in_=tq), [R_TQ], [R_KI])
            op("dve", lambda e: e.tensor_copy(out=tq, in_=ki), [R_KI], [R_TQ])
            op("dve", lambda e: e.scalar_tensor_tensor(out=rr, in0=tq, scalar=-TWO_PI, in1=ang, op0=ALU.mult, op1=ALU.add), [R_TQ, R_ANG], [R_RR])
            op("dve", lambda e: e.tensor_scalar(out=tq, in0=rr, scalar1=math.pi, scalar2=-math.pi, op0=ALU.min, op1=ALU.max), [R_RR], [R_TQ])
            op("act", lambda e, n0=n0: e.activation(out=sinT[:, n0:n0 + NB, :], in_=tq, func=AF.Sin), [R_TQ], [("sinT",)])
            op("dve", lambda e: e.tensor_scalar(out=ang, in0=rr, scalar1=math.pi / 2, scalar2=None, op0=ALU.add), [R_RR], [R_ANG])
            op("dve", lambda e: e.tensor_scalar(out=tq, in0=ang, scalar1=math.pi, scalar2=-TWO_PI, op0=ALU.is_gt, op1=ALU.mult), [R_ANG, ("sinT",)], [R_TQ])
            op("dve", lambda e: e.tensor_tensor(out=rr, in0=ang, in1=tq, op=ALU.add), [R_ANG, R_TQ], [R_RR])
            op("dve", lambda e: e.tensor_scalar(out=rr, in0=rr, scalar1=math.pi, scalar2=-math.pi, op0=ALU.min, op1=ALU.max), [R_RR], [R_RR])
            op("act", lambda e, n0=n0: e.activation(out=cosT[:, n0:n0 + NB, :], in_=rr, func=AF.Sin), [R_RR], [("cosT",)])

    tables_built = [False]

    pe_ctr = {"pa": 0, "rot": 0, "sbf": 0, "wug": 0, "retb": 0, "dn": 0}
    pending = []
    wug_slot_of = {}
    tmpA_b = tmpA.bitcast(BF16)
    RTMP_ALL = [("tmpA", b, i) for b in range(2) for i in range(3)]

    def flush_pending(keep=0):
        while len(pending) > keep:
            pending.pop(0)()

    def norm_stages(t, s, m, wb_tile, wb_key, xslot=None, sq_qrot=False):
        c = stat_col(3)
        ssa = stat[:m, c:c + 1]; aa = stat[:m, c + 1:c + 2]; ra = stat[:m, c + 2:c + 3]
        RS = ("stat", c)
        hbi = s % 2
        xs_ = cur["xs"][s] if xslot is None else xslot

        def st0():
            if sq_qrot:
                op("act", lambda e: e.activation(out=qrot.rearrange("p a c -> p (a c)")[:m, :], in_=x_sb[:m, xs_, :], func=AF.Square, accum_out=ssa),
                   [("x", xs_), RS], [RS] + [("qrot", a, b) for a in range(2) for b in range(2)])
            else:
                op("act", lambda e: e.activation(out=hb[:m, hbi, :], in_=x_sb[:m, xs_, :], func=AF.Square, accum_out=ssa), [("x", xs_), RS], [RS, ("hb", hbi)])

        def st1():
            op("dve", lambda e: e.tensor_scalar(out=aa, in0=ssa, scalar1=1.0 / D, scalar2=EPS, op0=ALU.mult, op1=ALU.add), [RS], [RS])

        def st2():
            op("pool", lambda e: e.tensor_tensor(out=ra, in0=aa, in1=negh[:m, 0:1], op=ALU.pow), [RS, ("negh",)], [RS])

        def st3():
            op("dve", lambda e: e.scalar_tensor_tensor(out=hb[:m, hbi, :], in0=x_sb[:m, xs_, :], scalar=ra, in1=wb_tile[:m, :],
                                                       op0=ALU.mult, op1=ALU.mult),
               [("x", xs_), RS, wb_key], [("hb", hbi)])

        def part_b():
            RT = ("pT",)
            for k in range(8):
                op("pe", lambda e, k=k: e.transpose(out=pT[:, k * 128: k * 128 + m], in_=hb[:m, hbi, k * 128:(k + 1) * 128], identity=ident[:m, :m]),
                   [("hb", hbi), ("ident",)], [RT])
            src = pT.rearrange("p (k c) -> p k c", c=128)[:, :, :m]
            dst = hT[:, :, s * 128:s * 128 + m]
            op("dve", lambda e: e.tensor_copy(out=dst, in_=src), [RT], [("hT", s)])
        return [st0, st1, st2, st3, part_b]

    def norm_to_hT(t, s, m, wb_tile, wb_key):
        st = norm_stages(t, s, m, wb_tile, wb_key)
        for f in st[:4]:
            f()
        return st[4]

    def load_slab_into(g, slot_):
        dma("sp", wslab[:, slot_], win_s[g], [("win_s", g)], [("wslab", slot_)], ("wslab", slot_))
        return slot_

    def proj_tokmajor(t, g, slot, ns, msz, tabidx, pre_hooks=None):
        for s in range(ns):
            m = msz[s]
            if pre_hooks is not None:
                pre_hooks[s]()
            bank = pe_ctr["pa"] % 2; pe_ctr["pa"] += 1
            RB = ("pb", bank)
            for k in range(8):
                op("pe", lambda e, k=k, s=s, m=m, bank=bank: e.matmul(pb[bank][:m, :], lhsT=hT[:, k, s * 128:s * 128 + m],
                                                                       rhs=wslab[:, slot, k, :], start=(k == 0), stop=(k == 7)),
                   [("hT", s), ("wslab", slot)], [RB])
            flush_pending(keep=1)
            pq = pb[bank]
            if g in (0, 1):
                ti = tabidx[s]
                rb = pe_ctr["rot"] % 2; pe_ctr["rot"] += 1
                pqv = pq[:m, :].rearrange("p (h two d) -> p h two d", h=H, two=2)
                cb = cosT[:m, ti, :].unsqueeze(1).unsqueeze(1).broadcast_to([m, H, 2, 64])
                sbb = sinT[:m, ti, :].unsqueeze(1).broadcast_to([m, H, 64])
                t1 = tmpA_b[:m, rb * 1024:rb * 1024 + 512].rearrange("p (h two d) -> p h two d", h=H, two=2)
                t2 = tmpA_b[:m, rb * 1024 + 512:(rb + 1) * 1024].rearrange("p (h two d) -> p h two d", h=H, two=2)
                RT0, RT1, RT2 = ("tmpA", rb, 0), ("tmpA", rb, 1), ("tmpA", rb, 2)
                op("dve", lambda e, pqv=pqv, cb=cb, t1=t1: e.tensor_tensor(out=t1, in0=pqv, in1=cb, op=ALU.mult), [RB, ("cosT",)], [RT0])
                op("dve", lambda e, pqv=pqv, sbb=sbb, t2=t2: e.tensor_tensor(out=t2[:, :, 0, :], in0=pqv[:, :, 1, :], in1=sbb, op=ALU.mult),
                   [RB, ("sinT",)], [RT1])
                op("dve", lambda e, pqv=pqv, sbb=sbb, t2=t2: e.tensor_tensor(out=t2[:, :, 1, :], in0=pqv[:, :, 0, :], in1=sbb, op=ALU.mult),
                   [RB, ("sinT",)], [RT2])
                qi = rb
                qr = qrot[:m, qi, :].rearrange("p (h two d) -> p h two d", h=H, two=2)
                RQ0, RQ1 = ("qrot", qi, 0), ("qrot", qi, 1)
                op("pool", lambda e, t1=t1, t2=t2, qr=qr: e.tensor_tensor(out=qr[:, :, 0, :], in0=t1[:, :, 0, :], in1=t2[:, :, 0, :], op=ALU.subtract),
                   [RT0, RT1], [RQ0])
                op("pool", lambda e, t1=t1, t2=t2, qr=qr: e.tensor_tensor(out=qr[:, :, 1, :], in0=t1[:, :, 1, :], in1=t2[:, :, 1, :], op=ALU.add),
                   [RT0, RT2], [RQ1])
                if g == 1:
                    tix = 0 if t == 0 else 1
                    teb = te_t[:m, tix, :].unsqueeze(2).broadcast_to([m, H, 128])
                    kdst = kte[:m, s, :].rearrange("p (h d) -> p h d", h=H)
                    qsrc = qrot[:m, qi, :].rearrange("p (h d) -> p h d", h=H)
                    op("pool", lambda e, kdst=kdst, qsrc=qsrc, teb=teb: e.tensor_tensor(out=kdst, in0=qsrc, in1=teb, op=ALU.mult),
                       [RQ0, RQ1, ("te_t",)], [("kte", s)])
                    tsrc_all = kte[:m, s, :]
                    RSRC = [("kte", s)]
                    dstT = kT
                    RD = ("kT", s)
                else:
                    tsrc_all = qrot[:m, qi, :]
                    RSRC = [RQ0, RQ1]
                    dstT = qT
                    RD = ("qT", s)

                def tr(tsrc_all=tsrc_all, RSRC=RSRC, dstT=dstT, RD=RD, m=m, s=s):
                    RT = ("pT",)
                    for h in range(H):
                        op("pe", lambda e, h=h: e.transpose(out=pT[:, h * 128: h * 128 + m], in_=tsrc_all[:, h * 128:(h + 1) * 128], identity=ident[:m, :m]),
                           RSRC + [("ident",)], [RT])
                    src = pT[:, 0:512].rearrange("p (k c) -> p k c", c=128)[:, :, :m]
                    dst = dstT[:, :, s * 128:s * 128 + m]
                    op("act", lambda e: e.activation(out=dst, in_=src, func=AF.Copy), [RT], [RD])
                pending.append(tr)
            elif g == 2:
                op("act", lambda e, pq=pq, s=s, m=m: e.activation(out=v_sb[:m, s, :], in_=pq[:m, :], func=AF.Copy), [RB], [("v_sb", s)])
            else:
                op("act", lambda e, pq=pq, s=s, m=m: e.activation(out=gs[:m, s, :], in_=pq[:m, :], func=AF.Silu), [RB], [("gs", s)])
                op("pool", lambda e, s=s, m=m: e.tensor_tensor(out=gs[:m, s, :], in0=gs[:m, s, :], in1=wb_ret[:m, :], op=ALU.mult),
                   [("gs", s), ("wb_ret",)], [("gs", s)])

    def retention_parts(t, s, m):
        tix = 0 if t == 0 else 1
        cols = slice(s * 128, s * 128 + m)
        R3, R4, R5 = ("pb", 3), ("pb", 4), ("pb", 5)
        sb_cur = pe_ctr["sbf"] % 2; pe_ctr["sbf"] += 1
        sb_nxt = 1 - sb_cur
        c = stat_col(12)
        ssh = stat[:m, c:c + 4]; ah = stat[:m, c + 4:c + 8]; rh = stat[:m, c + 8:c + 12]
        RS = ("stat", c)
        rbi = pe_ctr["retb"] % 2; pe_ctr["retb"] += 1

        def part_a():
            for h in range(H):
                op("pe", lambda e, h=h: e.matmul(pb[3][:m, h * 128:h * 128 + m], lhsT=kT[:, h, cols], rhs=qT[:, h, cols], start=True, stop=True),
                   [("kT", s), ("qT", s)], [R3])
            for h in range(H):
                op("pe", lambda e, h=h: e.matmul(pb[5][:, h * 128:(h + 1) * 128], lhsT=kte[:m, s, h * 128:(h + 1) * 128], rhs=v_sb[:m, s, h * 128:(h + 1) * 128],
                                                 start=True, stop=True), [("kte", s), ("v_sb", s)], [R5])
            scv = pb[3][:m, :].rearrange("p (h c) -> p h c", h=H)[:, :, :m]
            op("dve", lambda e: e.tensor_tensor(out=Pm[:m, :, :m], in0=scv, in1=(mask2s if t == 0 else mask2)[:m, :, :m], op=ALU.mult), [R3, ("mask2",)], [("Pm",)])
            for h in range(H):
                gval = math.exp(LG[h] * (16 if t == 0 else 128))
                op("dve", lambda e, h=h, gval=gval: e.scalar_tensor_tensor(out=S[:, h, :], in0=S[:, h, :], scalar=gval, in1=pb[5][:, h * 128:(h + 1) * 128],
                                                                           op0=ALU.mult, op1=ALU.add), [("S", h), R5], [("S", h)])
            op("dve", lambda e: e.tensor_copy(out=S_bf[:, sb_nxt], in_=S), [("S", h) for h in range(H)], [("S_bf", sb_nxt)])

        def part_b():
            for h in range(H):
                op("pe", lambda e, h=h: e.matmul(pb[4][:m, h * 128:(h + 1) * 128], lhsT=Pm[:m, h, :m], rhs=v_sb[:m, s, h * 128:(h + 1) * 128],
                                                 start=True, stop=False), [("Pm",), ("v_sb", s)], [R4])
                op("pe", lambda e, h=h: e.matmul(pb[4][:m, h * 128:(h + 1) * 128], lhsT=qT[:, h, cols], rhs=S_bf[:, sb_cur, h, :],
                                                 start=False, stop=True), [("qT", s), ("S_bf", sb_cur)], [R4])
            for h in range(H):
                op("act", lambda e, h=h: e.activation(out=ret_b[:m, rbi, h * 128:(h + 1) * 128], in_=pb[4][:m, h * 128:(h + 1) * 128], func=AF.Square,
                                                      accum_out=ssh[:, h:h + 1]), [R4, RS], [RS, ("ret_b", rbi, h)])
            op("dve", lambda e: e.scalar_tensor_tensor(out=ah, in0=ssh, scalar=1.0 / 128.0, in1=epsfs[:m, tix, :], op0=ALU.mult, op1=ALU.add),
               [RS, ("epsfs",)], [RS])
            op("pool", lambda e: e.tensor_tensor(out=rh, in0=ah, in1=negh[:m, 0:4], op=ALU.pow), [RS, ("negh",)], [RS])

        def part_c():
            for h in range(H):
                op("dve", lambda e, h=h: e.scalar_tensor_tensor(out=ret_b[:m, rbi, h * 128:(h + 1) * 128], in0=pb[4][:m, h * 128:(h + 1) * 128],
                                                                scalar=rh[:, h:h + 1], in1=gs[:m, s, h * 128:(h + 1) * 128], op0=ALU.mult, op1=ALU.mult),
                   [R4, RS, ("gs", s)], [("ret_b", rbi, h)])

            def tr():
                RT = ("pT",)
                for h in range(H):
                    op("pe", lambda e, h=h: e.transpose(out=pT[:, h * 128: h * 128 + m], in_=ret_b[:m, rbi, h * 128:(h + 1) * 128],
                                                        identity=ident[:m, :m]), [("ret_b", rbi, h), ("ident",)], [RT])
                src = pT[:, 0:512].rearrange("p (k c) -> p k c", c=128)[:, :, :m]
                op("act", lambda e: e.activation(out=mixT[:, 0:4, cols], in_=src, func=AF.Copy), [RT], [("mixTr", s)])
            pending.append(tr)
        return part_a, part_b, part_c

    def post_stages(src_ap, src_reads, s, m, wb_tile, wb_key):
        c = stat_col(3)
        ssa = stat[:m, c:c + 1]; aa = stat[:m, c + 1:c + 2]; ra = stat[:m, c + 2:c + 3]
        RS = ("stat", c)
        xs_ = cur["xs"][s]

        def p0():
            op("act", lambda e: e.activation(out=tmpA_b[:m, 0:1024], in_=src_ap, func=AF.Square, accum_out=ssa), list(src_reads) + [RS], [RS] + RTMP_ALL)

        def p1():
            op("dve", lambda e: e.tensor_scalar(out=aa, in0=ssa, scalar1=1.0 / D, scalar2=EPS, op0=ALU.mult, op1=ALU.add), [RS], [RS])

        def p2():
            op("pool", lambda e: e.tensor_tensor(out=ra, in0=aa, in1=negh[:m, 0:1], op=ALU.pow), [RS, ("negh",)], [RS])

        def p3():
            op("dve", lambda e: e.scalar_tensor_tensor(out=tmpA[:m, :], in0=src_ap, scalar=ra, in1=wb_tile[:m, :], op0=ALU.mult, op1=ALU.mult),
               list(src_reads) + [RS, wb_key], RTMP_ALL)

        def p4():
            op("dve", lambda e: e.tensor_tensor(out=x_sb[:m, xs_, :], in0=x_sb[:m, xs_, :], in1=tmpA[:m, :], op=ALU.add), RTMP_ALL + [("x", xs_)], [("x", xs_)])
        return [p0, p1, p2, p3, p4]

    def post_norm_residual(src_ap, src_reads, s, m, wb_tile, wb_key):
        for f_ in post_stages(src_ap, src_reads, s, m, wb_tile, wb_key):
            f_()

    def load_wug(t, c):
        wslot = pe_ctr["wug"] % NWUG; pe_ctr["wug"] += 1
        wug_slot_of[(t, c)] = wslot
        dma("sp", wug_sb[:, wslot], wug_s[c], [("wug_s", c, 0), ("wug_s", c, 1)], [("wug", wslot)], ("wug", wslot))

    def featmajor_mm(slot_, j, nt):
        bank = pe_ctr["pa"] % 2; pe_ctr["pa"] += 1
        for k in range(8):
            op("pe", lambda e, k=k: e.matmul(pb[bank][:, :nt], lhsT=wslab[:, slot_, k, j * 128:(j + 1) * 128],
                                             rhs=hT[:, k, :nt], start=(k == 0), stop=(k == 7)),
               [("hT", s) for s in range(4)] + [("wslab", slot_)], [("pb", bank)])
        return bank

    def conv_ops(j, nt):
        yi = j % 2
        zb, zk = cur["z"], cur["zk"]
        t1 = tmpA[:, 0:nt]
        t0 = tmpA[:, 512:512 + nt]
        R1 = [("tmpA", 0, i) for i in range(3)]
        R0 = [("tmpA", 1, i) for i in range(3)]
        op("act", lambda e: e.activation(out=yc[:, yi, :nt], in_=zb[:, j, 2:2 + nt], func=AF.Identity, scale=cw[:, j, 2:3]), [(zk, j), ("cw",)], [("yc", yi)])
        op("act", lambda e: e.activation(out=t1, in_=zb[:, j, 1:1 + nt], func=AF.Identity, scale=cw[:, j, 1:2]), [(zk, j), ("cw",)], R1)
        op("act", lambda e: e.activation(out=t0, in_=zb[:, j, 0:nt], func=AF.Identity, scale=cw[:, j, 0:1]), [(zk, j), ("cw",)], R0)
        op("pool", lambda e: e.tensor_tensor(out=yc[:, yi, :nt], in0=yc[:, yi, :nt], in1=t1, op=ALU.add), R1 + [("yc", yi)], [("yc", yi)])
        op("pool", lambda e: e.tensor_tensor(out=yc[:, yi, :nt], in0=yc[:, yi, :nt], in1=t0, op=ALU.add), R0 + [("yc", yi)], [("yc", yi)])
        op("pool", lambda e: e.tensor_copy(out=zb[:, j, 0:2], in_=zb[:, j, nt:nt + 2]), [(zk, j)], [(zk, j)])

    def b_chunk(slot_b, j, nt, mid=None):
        yi = j % 2
        bank = featmajor_mm(slot_b, j, nt)
        if mid is not None:
            mid()
        op("dve", lambda e: e.tensor_tensor(out=mixT[:, 4 + j, :nt], in0=pb[bank][:, :nt], in1=yc[:, yi, :nt], op=ALU.mult),
           [("pb", bank), ("yc", yi)], [("mixT", 4 + j)])

    tile_order = list(range(1, 1 + NT_PROMPT)) + [0]
    prefetched = {}

    def tile_params(t):
        i = tile_order.index(t)
        if t == 0:
            nt, ns, msz, tabidx = DEC, 1, [DEC], [0]
            xsrc = [xs]
            ydst = [ys]
        else:
            nt, ns, msz = TT, 4, [128] * 4
            base = (t - 1) * TT
            tabidx = [1 + (t - 1) * 4 + s for s in range(4)]
            xsrc = [xp[base + s * 128: base + (s + 1) * 128, :] for s in range(4)]
            ydst = [yp[base + s * 128: base + (s + 1) * 128, :] for s in range(4)]
        slots = [(4 * i + s) % 5 for s in range(ns)]
        return dict(t=t, nt=nt, ns=ns, msz=msz, tabidx=tabidx, xsrc=xsrc, ydst=ydst, slots=slots)

    def load_x(tp, s):
        sl = tp["slots"][s]
        dma("sp", x_sb[:tp["msz"][s], sl, :], tp["xsrc"][s], [], [("x", sl)], ("x", sl))
        prefetched[("x", tp["t"], s)] = True

    n1_recs = {}

    def n1_rec(t, s, tp):
        if (t, s) not in n1_recs:
            n1_recs[(t, s)] = {"st": norm_stages(t, s, tp["msz"][s], wb_pre_mix, ("wb_pre_mix",), xslot=tp["slots"][s], sq_qrot=(s >= 2)), "done": 0}
        return n1_recs[(t, s)]

    def adv(rec, upto):
        while rec["done"] < upto:
            rec["st"][rec["done"]]()
            rec["done"] += 1

    def run_tile(t):
        tp = tile_params(t)
        nt, ns, msz, tabidx, ydst = tp["nt"], tp["ns"], tp["msz"], tp["tabidx"], tp["ydst"]
        i = tile_order.index(t)
        tp_next = tile_params(tile_order[i + 1]) if i + 1 < len(tile_order) else None
        if t == 0:
            cur.update(z=z_s, zk="z_s", hf=halo_s, hfk="halo_s")
        else:
            cur.update(z=z, zk="z", hf=halo_f, hfk="halo_f")
        cur["xs"] = tp["slots"]
        for s in range(ns):
            if not prefetched.get(("x", t, s)):
                load_x(tp, s)
        recs = [n1_rec(t, s, tp) for s in range(ns)]
        for p0 in range(0, ns, 2):
            grp = list(range(p0, min(p0 + 2, ns)))
            for stage in range(1, 4):
                for s in grp:
                    adv(recs[s], stage)
            if p0 == 0:
                issue_casts(8)
                if not tables_built[0]:
                    tables_built[0] = True
                    build_tables()
        slot = 0 if prefetched.get(("slab", t)) else load_slab_into(0, 0)
        for s in range(min(2, ns)):
            adv(recs[s], 4)
        n1_hooks = {}
        for s in range(ns):
            def hook(s=s):
                adv(recs[s], 5)
                if s + 2 < ns:
                    adv(recs[s + 2], 4)
            n1_hooks[s] = hook
        order = [0, 1, 2, 3]
        for gi, g in enumerate(order):
            nxt = order[gi + 1] if gi + 1 < 4 else 6
            if gi == 0 and prefetched.get(("slab1", t)):
                nslot = 1 - slot
            else:
                nslot = load_slab_into(nxt, 1 - slot)
            proj_tokmajor(t, g, slot, ns, msz, tabidx, n1_hooks if gi == 0 else None)
            slot = nslot
            issue_casts(6)
        flush_pending()
        slot_hc = slot
        slot_c = load_slab_into(5, 1 - slot_hc)
        dma("sp", wout_sb, wout_s, [("wout_s",)], [("facc", s, hh) for s in range(4) for hh in range(2)], ("wout",))
        for j in range(4):
            bank = featmajor_mm(slot_hc, j, nt)
            hi = j % 2
            op("act", lambda e, bank=bank, hi=hi: e.activation(out=hc_sb[:, hi, :nt], in_=pb[bank][:, :nt], func=AF.Copy), [("pb", bank)], [("hc_sb", hi)])
            bank2 = featmajor_mm(slot_c, j, nt)
            op("dve", lambda e, bank2=bank2, hi=hi, j=j, zb=cur["z"]: e.tensor_tensor(out=zb[:, j, 2:2 + nt], in0=pb[bank2][:, :nt], in1=hc_sb[:, hi, :nt], op=ALU.mult),
               [("pb", bank2), ("hc_sb", hi)], [(cur["zk"], j)])
        slot_b = load_slab_into(4, slot_hc)
        issue_casts(6)
        load_wug(t, 0)
        load_wug(t, 1)
        dma("sp", wdn_sb, wdn_s[0], [("wdn_s", 0)], [("wdn_sb",)], ("wdn",))
        parts = [retention_parts(t, s, msz[s]) for s in range(ns)]
        parts[0][0]()
        parts[0][1]()
        if ns == 4:
            for s in range(ns):
                if s + 1 < ns:
                    parts[s + 1][0]()
                conv_ops(s, nt)
                b_chunk(slot_b, s, nt, mid=parts[s][2])
                flush_pending(keep=1)
                if s + 1 < ns:
                    parts[s + 1][1]()
            flush_pending()
        else:
            parts[0][2]()
            for j in range(4):
                conv_ops(j, nt)
                b_chunk(slot_b, j, nt)
            flush_pending()
        def wout_mm(s):
            m = msz[s]
            cols = slice(s * 128, s * 128 + m)
            if s % 2 == 0:
                pbuf, regs = p67, [("p67", 0), ("p67", 1)]
            else:
                pbuf, regs = p34, [("pb", 3), ("pb", 4)]
            for half in range(2):
                for k in range(8):
                    rk = ("mixTr", s) if k < 4 else ("mixT", k)
                    op("pe", lambda e, k=k, half=half: e.matmul(pbuf[:m, half * 512:(half + 1) * 512], lhsT=mixT[:, k, cols],
                                                                rhs=wout_sb[:, k, half * 512:(half + 1) * 512], start=(k == 0), stop=(k == 7)),
                       [rk] + [("facc", q, hh) for q in range(4) for hh in range(2)], [regs[half]])
            return post_stages(pbuf[:m, :], regs, s, m, wb_post_mix, ("wb_post_mix",))
        n2b = {}
        for s in range(ns + 2):
            post = wout_mm(s) if s < ns else []
            n2 = norm_stages(t, s - 1, msz[s - 1], wb_pre_ffn, ("wb_pre_ffn",)) if 1 <= s <= ns else []
            for k in range(5):
                if k < len(post):
                    post[k]()
                if n2 and k < 4:
                    n2[k]()
            if n2:
                n2b[s - 1] = n2[4]
            if 2 <= s:
                n2b[s - 2]()
        issue_casts(6)
        if tp_next is not None:
            load_x(tp_next, 0)
            load_slab_into(0, 0)
            prefetched[("slab", tile_order[i + 1])] = True
            load_slab_into(1, 1)
            prefetched[("slab1", tile_order[i + 1])] = True
        def emit_down(g):
                if g == 3 and tp_next is not None:
                    adv(n1_rec(tile_order[i + 1], 0, tp_next), 5)
                for s in range(ns):
                    m = msz[s]
                    cols = slice(s * 128, s * 128 + m)
                    for half in range(2):
                        di = pe_ctr["dn"] % 4; pe_ctr["dn"] += 1
                        RH, dbank = [(("p67", 0), p67[:, 0:512]), (("p67", 1), p67[:, 512:1024]), (("pb", 3), pb[3]), (("pb", 4), pb[4])][di]
                        hs = slice(half * 512, (half + 1) * 512)
                        for cc in range(8):
                            op("pe", lambda e, cc=cc, hs=hs, m=m, cols=cols, dbank=dbank: e.matmul(dbank[:m, :], lhsT=act[:, cc, cols], rhs=wdn_sb[:, cc, hs],
                                                                                                     start=(cc == 0), stop=(cc == 7)),
                               [("act", cc), ("wdn_sb",)], [RH])
                        if g == 0:
                            op("act", lambda e, s=s, m=m, hs=hs, dbank=dbank: e.activation(out=f_acc[:m, s, hs], in_=dbank[:m, :], func=AF.Copy), [RH], [("facc", s, half)])
                        else:
                            op("dve", lambda e, s=s, m=m, hs=hs, dbank=dbank: e.tensor_tensor(out=f_acc[:m, s, hs], in0=f_acc[:m, s, hs], in1=dbank[:m, :], op=ALU.add),
                               [RH, ("facc", s, half)], [("facc", s, half)])
                    if g == 3:
                        post = post_stages(f_acc[:m, s, :], [("facc", s, 0), ("facc", s, 1)], s, m, wb_post_ffn, ("wb_post_ffn",))
                        nrec = None
                        if tp_next is not None:
                            tn = tile_order[i + 1]
                            if 1 <= s - 2 < tp_next["ns"]:
                                adv(n1_rec(tn, s - 2, tp_next), 5)
                            if 1 <= s - 1 < tp_next["ns"]:
                                nrec = n1_rec(tn, s - 1, tp_next)
                        for k in range(5):
                            post[k]()
                            if nrec is not None and k < 4:
                                adv(nrec, k + 1)
                        dma("sp", ydst[s], x_sb[:m, tp["slots"][s], :], [("x", tp["slots"][s])], [], ("y", tp["slots"][s]), is_out=True)
                        if tp_next is not None and s + 1 < tp_next["ns"]:
                            load_x(tp_next, s + 1)

        for g in range(4):
            if g == 3 and tp_next is not None:
                adv(n1_rec(tile_order[i + 1], 0, tp_next), 4)
            for cc in range(8):
                c = g * 8 + cc
                if g > 0 and cc == 0 and t == 0:
                    emit_down(g - 1)
                wslot = wug_slot_of[(t, c)]
                if c + 2 < 32:
                    load_wug(t, c + 2)
                if g > 0 and cc == 1:
                    dma("sp", wdn_sb, wdn_s[g], [("wdn_s", g)], [("wdn_sb",)], ("wdn",))
                issue_casts(2 if cc else 3)
                if t == 0:
                    b3 = c % 3
                    PU, RU, PG, RG = [(pb[0], ("pb", 0), pb[1], ("pb", 1)), (pb[3], ("pb", 3), pb[4], ("pb", 4)),
                                      (pb[5], ("pb", 5), p67[:, 0:512], ("p67", 0))][b3]
                    if b3 < 2:
                        upv, RUP = up_sb[:, b3, :], ("upsb", b3)
                        accv, RACC = acc[:, b3, :], ("acc", b3)
                        glv, RGL = gl[:, b3, :], ("gl", b3)
                    else:
                        upv, RUP = hc_sb[:, 0, :], ("hc_sb", 0)
                        accv, RACC = hc_sb[:, 1, :], ("hc_sb", 1)
                        glv, RGL = yc.bitcast(BF16)[:, 0, :], ("yc", 0)
                else:
                    b = c % 2
                    PU, RU, PG, RG = [(pb[0], ("pb", 0), pb[1], ("pb", 1)), (pb[3], ("pb", 3), pb[4], ("pb", 4))][b]
                    upv, RUP = up_sb[:, b, :], ("upsb", b)
                    accv, RACC = acc[:, b, :], ("acc", b)
                    glv, RGL = gl[:, b, :], ("gl", b)
                hf, hfk = cur["hf"], cur["hfk"]
                for k in range(8):
                    op("pe", lambda e, k=k, wslot=wslot, PU=PU: e.matmul(PU[:, :nt], lhsT=wug_sb[:, wslot, 0, k, :], rhs=hT[:, k, :nt],
                                                                          start=(k == 0), stop=(k == 7)),
                       [("hT", s) for s in range(4)] + [("wug", wslot)], [RU])
                for k in range(8):
                    op("pe", lambda e, k=k, wslot=wslot, PG=PG: e.matmul(PG[:, :nt], lhsT=wug_sb[:, wslot, 1, k, :], rhs=hT[:, k, :nt],
                                                                          start=(k == 0), stop=(k == 7)),
                       [("hT", s) for s in range(4)] + [("wug", wslot)], [RG])
                if g > 0 and cc == 0 and t != 0:
                    emit_down(g - 1)
                RUPH = ("uph",) + RUP
                op("pool", lambda e, upv=upv, c=c, hf=hf: e.tensor_copy(out=upv[:, 0:2], in_=hf[:, c, :]), [(hfk,)], [RUPH])
                op("act", lambda e, upv=upv, PU=PU: e.activation(out=upv[:, 2:2 + nt], in_=PU[:, :nt], func=AF.Copy), [RU], [RUP])
                op("pool", lambda e, upv=upv, c=c, hf=hf: e.tensor_copy(out=hf[:, c, :], in_=upv[:, nt:nt + 2]), [RUP, (hfk,)], [(hfk,)])
                op("act", lambda e, accv=accv, PU=PU, c=c: e.activation(out=accv[:, :nt], in_=PU[:, :nt], func=AF.Identity, scale=fw[:, c, 2:3]),
                   [RU, ("fw",)], [RACC])
                op("dve", lambda e, accv=accv, upv=upv, c=c: e.scalar_tensor_tensor(out=accv[:, :nt], in0=upv[:, 1:1 + nt], scalar=fw[:, c, 1:2], in1=accv[:, :nt],
                                                                                    op0=ALU.mult, op1=ALU.add), [RUP, RUPH, RACC, ("fw",)], [RACC])
                op("dve", lambda e, accv=accv, upv=upv, c=c: e.scalar_tensor_tensor(out=accv[:, :nt], in0=upv[:, 0:nt], scalar=fw[:, c, 0:1], in1=accv[:, :nt],
                                                                                    op0=ALU.mult, op1=ALU.add), [RUP, RUPH, RACC, ("fw",)], [RACC])
                op("act", lambda e, accv=accv, glv=glv: e.activation(out=glv[:, :nt], in_=accv[:, :nt], func=AF.Gelu_apprx_tanh), [RACC], [RGL])
                op("dve", lambda e, glv=glv, PG=PG, cc=cc: e.tensor_tensor(out=act[:, cc, :nt], in0=PG[:, :nt], in1=glv[:, :nt], op=ALU.mult),
                   [RG, RGL], [("act", cc)])
        emit_down(3)

    def emit_cache_outputs(cc_o, cf_o, zb, zk, hf, hfk, tag):
        R5 = ("pb", 5)
        for r in range(2):
            op("pe", lambda e, r=r: e.transpose(out=pb[5][:32, r * 128:(r + 1) * 128], in_=hf[:, :, r], identity=ident_f),
               [(hfk,), ("ident_f",)], [R5])
            op("pe", lambda e, r=r: e.transpose(out=pb[5][:4, (2 + r) * 128:(3 + r) * 128], in_=zb[:, :, r], identity=ident_f),
               [(zk, j) for j in range(4)] + [("ident_f",)], [R5])
        op("act", lambda e: e.activation(out=tmpA[:32, 0:512], in_=pb[5][:32, :], func=AF.Copy), [R5], RTMP_ALL)
        for r in range(2):
            dma("sp", cf_o[r].rearrange("(c p) -> c p", p=128), tmpA[:32, r * 128:(r + 1) * 128], RTMP_ALL, [], ("so", tag), is_out=True)
            dma("sp", cc_o[r].rearrange("(j p) -> j p", p=128), tmpA[:4, (2 + r) * 128:(3 + r) * 128], RTMP_ALL, [], ("so", tag), is_out=True)

    SALL = [("S", h) for h in range(H)]
    op("dve", lambda e: e.memset(S, 0.0), [], SALL)
    op("dve", lambda e: e.memset(S_bf, 0.0), [], [("S_bf", 0), ("S_bf", 1)])
    op("dve", lambda e: e.memset(z[:, :, 0:2], 0.0), [], [("z", j) for j in range(4)])
    op("dve", lambda e: e.memset(halo_f, 0.0), [], [("halo_f",)])
    for r in range(2):
        dma("sp", z_s[:, :, r], cc_in[r].rearrange("(j p) -> p j", p=128), [], [("z_s", j) for j in range(4)], ("cstslow",), slow=True)
    for r in range(2):
        dma("sp", halo_s[:, :, r], cf_in[r].rearrange("(c p) -> p c", p=128), [], [("halo_s",)], ("cstslow",), slow=True)
    for t in tile_order[:-1]:
        run_tile(t)
    dma("sp", srp.rearrange("h d e -> d h e"), S, SALL, [], ("so", "p0"), is_out=True)
    dma("sp", S, st_in.rearrange("h d e -> d h e"), [], SALL, ("cst2",))
    sbn = pe_ctr["sbf"] % 2
    op("dve", lambda e: e.tensor_copy(out=S_bf[:, sbn], in_=S), SALL, [("S_bf", sbn)])
    run_tile(0)
    dma("sp", srs.rearrange("h d e -> d h e"), S, SALL, [], ("so", "s0"), is_out=True)
    emit_cache_outputs(ccp, cfp, z, "z", halo_f, "halo_f", "p")
    emit_cache_outputs(ccs, cfs, z_s, "z_s", halo_s, "halo_s", "s")
    P.emit()
    return nc


_CACHE = {}


def kernel(x_prompt, x_sample, state_ret, cache_conv, cache_ffn_conv,
           w_in, w_out, conv_w, ret_norm_w, pre_mix_w, post_mix_w,
           pre_ffn_w, post_ffn_w, w_up, w_gate, ffn_conv_w, w_down):
    f = lambda a: np.ascontiguousarray(np.asarray(a, dtype=np.float32))
    x_prompt = f(x_prompt); x_sample = f(x_sample); state_ret = f(state_ret)
    cache_conv = f(cache_conv); cache_ffn_conv = f(cache_ffn_conv)
    shared = {
        "w_in": f(w_in)[0], "w_out": f(w_out)[0], "conv_w": f(conv_w)[0], "ret_w": f(ret_norm_w),
        "pre_mix": f(pre_mix_w), "post_mix": f(post_mix_w), "pre_ffn": f(pre_ffn_w), "post_ffn": f(post_ffn_w),
        "w_up": f(w_up)[0], "w_gate": f(w_gate)[0], "ffn_cw": f(ffn_conv_w)[0], "w_down": f(w_down)[0],
    }
    if "nc" not in _CACHE:
        _CACHE["nc"] = build_program()
    nc = _CACHE["nc"]
    in_maps = []
    for c in range(NCORES):
        m = dict(shared)
        m["xp"] = x_prompt[c]; m["xs"] = x_sample[c]
        m["st"] = state_ret[0, c]; m["cc"] = cache_conv[0, c]; m["cf"] = cache_ffn_conv[0, c]
        in_maps.append(m)
    res = run_bass_kernel_spmd(nc, in_maps, core_ids=list(range(NCORES)))
    r = res.results
    st = lambda k: np.stack([np.asarray(r[c][k], dtype=np.float32) for c in range(NCORES)])
    yp = st("yp"); ys = st("ys")
    return (yp, ys, st("srp")[None], st("ccp")[None], st("cfp")[None],
            st("srs")[None], st("ccs")[None], st("cfs")[None])
```

```python
import math
import numpy as np
import concourse.bass as bass
import concourse.mybir as mybir
from concourse.bass_utils import run_bass_kernel_spmd

F32 = mybir.dt.float32
BF16 = mybir.dt.bfloat16
I32 = mybir.dt.int32
AF = mybir.ActivationFunctionType
ALU = mybir.AluOpType

D = 1024
SEQ = 4096
DEC = 16
PAST_LEN = 4096
TT = 512
NT_PROMPT = SEQ // TT
H = 4
DFF = 4096
EPS = 1e-6
NCORES = 8
LG = [math.log(1.0 - 2.0 ** (-5.0 - h)) for h in range(H)]
SC = 128.0 ** -0.5
TWO_PI = 2.0 * math.pi


GROUP_KEYS = ("castug", "cst", "cstslow", "cst2", "so")


class Prog:
    def __init__(self, nc):
        self.nc = nc
        self.ops = []
        self.last_w = {}
        self.readers = {}
        self.out_dma = []

    def add(self, eng, fn, reads=(), writes=(), dma_key=None, is_out=False):
        idx = len(self.ops)
        deps = set()
        for r in reads:
            if r in self.last_w:
                deps.add(self.last_w[r])
        for w in writes:
            if w in self.last_w:
                deps.add(self.last_w[w])
            for rd in self.readers.get(w, {}).values():
                deps.add(rd)
        for w in writes:
            self.last_w[w] = idx
            self.readers[w] = {}
        for r in reads:
            if r not in writes:
                rk = eng if dma_key is None else ("dma", idx)
                self.readers.setdefault(r, {})[rk] = idx
        deps.discard(idx)
        if dma_key is not None and dma_key[0] in GROUP_KEYS:
            deps = {d for d in deps if self.ops[d]["dma_key"] != dma_key}
        self.ops.append(dict(eng=eng, fn=fn, deps=deps, dma_key=dma_key, idx=idx))
        if is_out:
            self.out_dma.append(idx)
        return idx

    def emit(self):
        nc = self.nc
        ops = self.ops
        needed = set()
        for o in ops:
            for d in o["deps"]:
                p = ops[d]
                if p["eng"] == "pe" and o["eng"] == "pe" and p["dma_key"] is None and o["dma_key"] is None:
                    continue
                needed.add(d)
        eng_names = ["pe", "act", "dve", "pool", "sp"]
        counts = {e: 0 for e in eng_names}
        dma_counts = {}
        dma_keys = []
        for o in ops:
            if o["dma_key"] is not None:
                k = o["dma_key"]
                if k not in dma_counts:
                    dma_counts[k] = 0
                    dma_keys.append(k)
                dma_counts[k] += 16
                o["tok"] = (("dma", k), dma_counts[k])
                o["is_dma"] = True
            elif o["idx"] in needed:
                counts[o["eng"]] += 1
                o["tok"] = (("eng", o["eng"]), counts[o["eng"]])
            else:
                o["tok"] = None
        for o in ops:
            if o["dma_key"] is not None and o["dma_key"][0] in GROUP_KEYS:
                o["tok"] = (("dma", o["dma_key"]), dma_counts[o["dma_key"]])
        import contextlib
        with contextlib.ExitStack() as es:
            sems = {}
            for e in eng_names:
                sems[("eng", e)] = es.enter_context(nc.semaphore("s_" + e))
            for i, k in enumerate(dma_keys):
                sems[("dma", k)] = es.enter_context(nc.semaphore("d%d" % i))
            block = es.enter_context(nc.Block())
            per_eng = {e: [o for o in ops if o["eng"] == e] for e in eng_names}
            final_waits = []
            for k in dma_keys:
                final_waits.append((("dma", k), dma_counts[k]))

            def run(engname, eng, is_last=False):
                waited = {}
                for o in per_eng[engname]:
                    need = {}
                    for d in o["deps"]:
                        p = ops[d]
                        if p["tok"] is None:
                            continue
                        if p["eng"] == "pe" and engname == "pe" and p["dma_key"] is None and o["dma_key"] is None:
                            continue
                        sk, val = p["tok"]
                        if need.get(sk, 0) < val:
                            need[sk] = val
                    todo = [(sk, val) for sk, val in need.items() if waited.get(sk, 0) < val]
                    for sk, val in todo[:-1]:
                        eng.wait_ge(sems[sk], val)
                        waited[sk] = val
                    inst = o["fn"](eng)
                    if todo:
                        sk, val = todo[-1]
                        inst._wait_ge(sems[sk], val)
                        waited[sk] = val
                    if o["tok"] is not None:
                        sk, val = o["tok"]
                        inst.then_inc(sems[sk], 16 if sk[0] == "dma" else 1)
                if is_last:
                    for sk, val in final_waits:
                        eng.wait_ge(sems[sk], val)

            @block.tensor
            def _(e):
                run("pe", e)

            @block.scalar
            def _(e):
                run("act", e)

            @block.vector
            def _(e):
                run("dve", e)

            @block.gpsimd
            def _(e):
                run("pool", e)

            @block.sync
            def _(e):
                run("sp", e, is_last=True)


def build_program():
    nc = bass.Bass("TRN2", target_bir_lowering=False)
    P = Prog(nc)

    def din(name, shape):
        return nc.dram_tensor(name, shape, F32, kind="ExternalInput").ap()

    def dout(name, shape):
        return nc.dram_tensor(name, shape, F32, kind="ExternalOutput").ap()

    xp = din("xp", [SEQ, D]); xs = din("xs", [DEC, D])
    st_in = din("st", [H, 128, 128]); cc_in = din("cc", [2, 512]); cf_in = din("cf", [2, DFF])
    w_in = din("w_in", [D, 3584]); w_out = din("w_out", [D, D])
    conv_w = din("conv_w", [3, 512]); ret_w = din("ret_w", [1, 512])
    pre_mix = din("pre_mix", [1, D]); post_mix = din("post_mix", [1, D])
    pre_ffn = din("pre_ffn", [1, D]); post_ffn = din("post_ffn", [1, D])
    w_up = din("w_up", [D, DFF]); w_gate = din("w_gate", [D, DFF])
    ffn_cw = din("ffn_cw", [3, DFF]); w_down = din("w_down", [DFF, D])
    yp = dout("yp", [SEQ, D]); ys = dout("ys", [DEC, D])
    srp = dout("srp", [H, 128, 128]); ccp = dout("ccp", [2, 512]); cfp = dout("cfp", [2, DFF])
    srs = dout("srs", [H, 128, 128]); ccs = dout("ccs", [2, 512]); cfs = dout("cfs", [2, DFF])

    win_s = nc.dram_tensor("win_s", [7, 128, 8, 512], BF16).ap()
    wout_s = nc.dram_tensor("wout_s", [128, 8, 1024], BF16).ap()
    wug_s = nc.dram_tensor("wug_s", [32, 128, 2, 8, 128], BF16).ap()
    wdn_s = nc.dram_tensor("wdn_s", [4, 128, 8, 1024], BF16).ap()

    def sb(name, shape, dt):
        return nc.alloc_sbuf_tensor(name, shape, dt).ap()

    x_sb = sb("x_sb", [128, 5, D], F32)
    hT = sb("hT", [128, 8, TT], BF16)
    wslab = sb("wslab", [128, 2, 8, 512], BF16)
    f_acc = sb("f_acc", [128, 4, D], F32)
    wout_sb = f_acc.bitcast(BF16).rearrange("p a (b c) -> p (a b) c", c=1024)
    NWUG = 3
    wug_sb = sb("wug_sb", [128, NWUG, 2, 8, 128], BF16)
    qT = sb("qT", [128, H, TT], BF16); kT = sb("kT", [128, H, TT], BF16)
    kte = sb("kte", [128, 4, 512], BF16); v_sb = sb("v_sb", [128, 4, 512], BF16); gs = sb("gs", [128, 4, 512], BF16)
    qrot = sb("qrot", [128, 2, 512], BF16)
    tmpA = sb("tmpA", [128, 1024], F32)
    z = sb("z", [128, 4, TT + 2], F32)
    hc_sb = sb("hc_sb", [128, 2, TT], F32); yc = sb("yc", [128, 2, TT], F32)
    mixT = sb("mixT", [128, 8, TT], BF16)
    up_sb = sb("up_sb", [128, 2, TT + 2], F32); acc = sb("acc", [128, 2, TT], F32); gl = sb("gl", [128, 2, TT], BF16)
    act = sb("act", [128, 8, TT], BF16)
    wdn_sb = sb("wdn_sb", [128, 8, 1024], BF16)
    wb_pre_mix = sb("wb_pre_mix", [128, D], F32); wb_post_mix = sb("wb_post_mix", [128, D], F32)
    wb_pre_ffn = sb("wb_pre_ffn", [128, D], F32); wb_post_ffn = sb("wb_post_ffn", [128, D], F32)
    wb_ret = sb("wb_ret", [128, 512], F32)
    cosT = sb("cosT", [128, 33, 64], F32); sinT = sb("sinT", [128, 33, 64], F32)
    mask2 = sb("mask2", [128, H, 128], F32)
    mask2s = sb("mask2s", [128, H, DEC], F32)
    ident_f = sb("ident_f", [128, 128], F32)
    te_t = sb("te_t", [128, 2, H], F32); epsfs = sb("epsfs", [128, 2, H], F32)
    S = sb("S", [128, H, 128], F32); S_bf = sb("S_bf", [128, 2, H, 128], BF16)
    Pm = sb("Pm", [128, H, 128], BF16)
    ret_b = sb("ret_b", [128, 2, 512], BF16)
    hb = sb("hb", [128, 2, D], BF16)
    halo_f = sb("halo_f", [128, 32, 2], F32)
    z_s = sb("z_s", [128, 4, DEC + 2], F32)
    halo_s = sb("halo_s", [128, 32, 2], F32)
    cur = {"z": z, "zk": "z", "hf": halo_f, "hfk": "halo_f", "xs": [0, 1, 2, 3]}
    cw = sb("cw", [128, 4, 3], F32); fw = sb("fw", [128, 32, 3], F32)
    ident = sb("ident", [128, 128], BF16)
    negh = sb("negh", [128, 8], F32)
    stat = sb("stat", [128, 96], F32)
    inv_f = sb("inv_f", [128, 64], F32)
    pos_fp = sb("pos_fp", [128, 33], F32)
    ptmp = wdn_sb.bitcast(F32).rearrange("p a c -> p (a c)")

    def ps(name, shape, dt):
        return nc.alloc_psum_tensor(name, shape, dt).ap()

    p34 = ps("p34", [128, 1024], F32)
    pb = [ps("pb0", [128, 512], F32), ps("pb1", [128, 512], F32), None,
          p34[:, 0:512], p34[:, 512:1024], ps("pb5", [128, 512], F32)]
    pT = ps("pT", [128, 1024], BF16)
    p67 = ps("p67", [128, 1024], F32)

    def dma(q, out, in_, reads, writes, key, is_out=False, slow=False):
        def fn(e, out=out, in_=in_):
            if slow:
                return e.dma_start(out=out, in_=in_, allow_slow_non_contiguous=True)
            return e.dma_start(out=out, in_=in_)
        P.add(q, fn, reads, writes, dma_key=key, is_out=is_out)

    def op(eng, f, reads, writes):
        reads = list(reads)
        if any(k[0] == "pt" for k in list(reads) + list(writes)):
            reads.append(("wdn_sb",))
        P.add(eng, f, reads, writes)

    stat_ctr = [0]

    def stat_col(n=1):
        assert n <= 12
        k = stat_ctr[0] % 8
        stat_ctr[0] += 1
        return 12 * k

    w_in_v = w_in.rearrange("(k p) (g c) -> g p k c", p=128, c=512)
    w_up_v = w_up.rearrange("(k p) (c j) -> c p k j", p=128, j=128)
    w_gate_v = w_gate.rearrange("(k p) (c j) -> c p k j", p=128, j=128)
    w_dn_v = w_down.rearrange("(g cc p) n -> g p cc n", cc=8, p=128)
    cast_queue = []
    for g in range(7):
        cast_queue.append(lambda g=g: dma("pool", win_s[g], w_in_v[g], [], [("win_s", g)], ("castx", "in", g)))
    cast_queue.append(lambda: dma("pool", wout_s, w_out.rearrange("(k p) c -> p k c", p=128), [], [("wout_s",)], ("castx", "out")))
    for g in range(4):
        for c2 in range(8):
            c0 = g * 8 + c2
            cast_queue.append(lambda c0=c0, g=g: dma("pool", wug_s[c0, :, 0], w_up_v[c0], [], [("wug_s", c0, 0)], ("castug", c0)))
            cast_queue.append(lambda c0=c0, g=g: dma("pool", wug_s[c0, :, 1], w_gate_v[c0], [], [("wug_s", c0, 1)], ("castug", c0)))
        cast_queue.append(lambda g=g: dma("pool", wdn_s[g], w_dn_v[g], [], [("wdn_s", g)], ("castx", "d", g)))

    def issue_casts(n):
        for _ in range(n):
            if cast_queue:
                cast_queue.pop(0)()

    dma("sp", wb_pre_mix, pre_mix.partition_broadcast(128), [], [("wb_pre_mix",)], ("cst",))
    dma("sp", wb_post_mix, post_mix.partition_broadcast(128), [], [("wb_post_mix",)], ("cst",))
    dma("sp", wb_pre_ffn, pre_ffn.partition_broadcast(128), [], [("wb_pre_ffn",)], ("cst",))
    dma("sp", wb_post_ffn, post_ffn.partition_broadcast(128), [], [("wb_post_ffn",)], ("cst",))
    dma("sp", wb_ret, ret_w.partition_broadcast(128), [], [("wb_ret",)], ("cst",))
    for i in range(3):
        dma("sp", cw[:, :, i], conv_w[i].rearrange("(j p) -> p j", p=128), [], [("cw",)], ("cstslow",), slow=True)
    for i in range(3):
        dma("sp", fw[:, :, i], ffn_cw[i].rearrange("(c p) -> p c", p=128), [], [("fw",)], ("cstslow",), slow=True)

    op("dve", lambda e: e.memset(negh, -0.5), [], [("negh",)])
    idi = ptmp[:, 0:128].bitcast(I32)
    idf = ptmp[:, 128:256]
    op("pool", lambda e: e.iota(idi, pattern=[[1, 128]], base=0, channel_multiplier=-1), [], [("pt", 0)])
    op("dve", lambda e: e.tensor_copy(out=idf, in_=idi), [("pt", 0)], [("pt", 1)])
    op("dve", lambda e: e.tensor_scalar(out=ident, in0=idf, scalar1=0.0, scalar2=None, op0=ALU.is_equal), [("pt", 1)], [("ident",)])
    op("dve", lambda e: e.tensor_scalar(out=ident_f, in0=idf, scalar1=0.0, scalar2=None, op0=ALU.is_equal), [("pt", 1)], [("ident_f",)])

    def build_tables():
        dji = ptmp[:, 256:384].bitcast(I32)
        djf = ptmp[:, 384:512]
        ef = ptmp[:, 512:640]
        pidx_i = ptmp[:, 640:641].bitcast(I32)
        pidx_f = ptmp[:, 641:642]
        tv = ptmp[:, 642:643]
        op("pool", lambda e: e.iota(dji, pattern=[[-1, 128]], base=0, channel_multiplier=1), [], [("pt", 2)])
        op("dve", lambda e: e.tensor_copy(out=djf, in_=dji), [("pt", 2)], [("pt", 3)])
        op("dve", lambda e: e.tensor_scalar(out=djf, in0=djf, scalar1=0.0, scalar2=2.0, op0=ALU.max, op1=ALU.mult), [("pt", 3)], [("pt", 3)])
        op("pool", lambda e: e.iota(pidx_i, pattern=[[0, 1]], base=0, channel_multiplier=1), [], [("pt", 4)])
        op("dve", lambda e: e.tensor_copy(out=pidx_f, in_=pidx_i), [("pt", 4)], [("pt", 5)])
        for ti, L in ((0, 16), (1, 128)):
            op("dve", lambda e, L=L: e.tensor_scalar(out=ef, in0=djf, scalar1=float(-L), scalar2=None, op0=ALU.add), [("pt", 3)], [("pt", 6)])
            for h in range(H):
                if ti == 0:
                    op("act", lambda e, h=h: e.activation(out=mask2s[:, h, :], in_=ef[:, 0:DEC], func=AF.Exp, scale=LG[h]), [("pt", 6)], [("mask2",)])
                else:
                    op("act", lambda e, h=h: e.activation(out=mask2[:, h, :], in_=ef, func=AF.Exp, scale=LG[h]), [("pt", 6)], [("mask2",)])
            op("dve", lambda e, L=L: e.tensor_scalar(out=tv, in0=pidx_f, scalar1=-1.0, scalar2=float(L - 1), op0=ALU.mult, op1=ALU.add), [("pt", 5)], [("pt", 7)])
            for h in range(H):
                op("act", lambda e, ti=ti, h=h: e.activation(out=te_t[:, ti, h:h + 1], in_=tv, func=AF.Exp, scale=LG[h]), [("pt", 7)], [("te_t",)])
            op("dve", lambda e: e.tensor_scalar(out=tv, in0=pidx_f, scalar1=1.0, scalar2=None, op0=ALU.add), [("pt", 5)], [("pt", 7)])
            for h in range(H):
                op("act", lambda e, ti=ti, h=h: e.activation(out=epsfs[:, ti, h:h + 1], in_=tv, func=AF.Exp, scale=-2.0 * LG[h]), [("pt", 7)], [("epsfs",)])
        op("dve", lambda e: e.tensor_scalar(out=te_t, in0=te_t, scalar1=SC, scalar2=None, op0=ALU.mult), [("te_t",)], [("te_t",)])
        op("dve", lambda e: e.tensor_scalar(out=epsfs, in0=epsfs, scalar1=EPS, scalar2=None, op0=ALU.mult), [("epsfs",)], [("epsfs",)])
        op("dve", lambda e: e.memset(mask2[64:128, :, 0:64], 0.0), [("mask2",)], [("mask2",)])

        for i in range(64):
            op("dve", lambda e, i=i: e.memset(inv_f[:, i:i + 1], float(np.float32(10000.0 ** (-(2.0 * i) / 128.0)))), [], [("inv_f",)])
        pos_i = ptmp[:, 700:733].bitcast(I32)
        op("pool", lambda e: e.iota(pos_i[:, 1:33], pattern=[[128, 32]], base=0, channel_multiplier=1), [], [("pt", 8)])
        op("pool", lambda e: e.iota(pos_i[:, 0:1], pattern=[[0, 1]], base=PAST_LEN, channel_multiplier=1), [("pt", 8)], [("pt", 8)])
        op("dve", lambda e: e.tensor_copy(out=pos_fp, in_=pos_i), [("pt", 8)], [("pos_fp",)])
        NB = 11
        ang = ptmp[:, 1024:1024 + NB * 64].rearrange("p (n i) -> p n i", i=64)
        tq = ptmp[:, 1792:1792 + NB * 64].rearrange("p (n i) -> p n i", i=64)
        ki = ptmp[:, 2560:2560 + NB * 64].bitcast(I32).rearrange("p (n i) -> p n i", i=64)
        rr = ptmp[:, 3328:3328 + NB * 64].rearrange("p (n i) -> p n i", i=64)
        for f_ in rot_block_ops(0, NB, ang, tq, ki, rr, [("pt", 10), ("pt", 11), ("pt", 12), ("pt", 13)], ("rot", 0)):
            f_()

    def rot_block_ops(n0, nb, ang, tq, ki, rr, RK, tabkey):
        R_ANG, R_TQ, R_KI, R_RR = RK
        return [
            lambda: op("dve", lambda e: e.tensor_tensor(out=ang, in0=pos_fp[:, n0:n0 + nb].unsqueeze(2).broadcast_to([128, nb, 64]),
                                                        in1=inv_f.unsqueeze(1).broadcast_to([128, nb, 64]), op=ALU.mult),
                       [("pos_fp",), ("inv_f",)], [R_ANG]),
            lambda: op("dve", lambda e: e.tensor_scalar(out=tq, in0=ang, scalar1=1.0 / TWO_PI, scalar2=None, op0=ALU.mult), [R_ANG], [R_TQ]),
            lambda: op("dve", lambda e: e.tensor_copy(out=ki, in_=tq), [R_TQ], [R_KI]),
            lambda: op("dve", lambda e: e.tensor_copy(out=tq, in_=ki), [R_KI], [R_TQ]),
            lambda: op("dve", lambda e: e.scalar_tensor_tensor(out=rr, in0=tq, scalar=-TWO_PI, in1=ang, op0=ALU.mult, op1=ALU.add), [R_TQ, R_ANG], [R_RR]),
            lambda: op("dve", lambda e: e.tensor_scalar(out=tq, in0=rr, scalar1=math.pi, scalar2=-math.pi, op0=ALU.min, op1=ALU.max), [R_RR], [R_TQ]),
            lambda: op("act", lambda e: e.activation(out=sinT[:, n0:n0 + nb, :], in_=tq, func=AF.Sin), [R_TQ], [tabkey]),
            lambda: op("dve", lambda e: e.tensor_scalar(out=ang, in0=rr, scalar1=math.pi / 2, scalar2=None, op0=ALU.add), [R_RR], [R_ANG]),
            lambda: op("dve", lambda e: e.tensor_scalar(out=tq, in0=ang, scalar1=math.pi, scalar2=-TWO_PI, op0=ALU.is_gt, op1=ALU.mult), [R_ANG], [R_TQ]),
            lambda: op("dve", lambda e: e.tensor_tensor(out=rr, in0=ang, in1=tq, op=ALU.add), [R_ANG, R_TQ], [R_RR]),
            lambda: op("dve", lambda e: e.tensor_scalar(out=rr, in0=rr, scalar1=math.pi, scalar2=-math.pi, op0=ALU.min, op1=ALU.max), [R_RR], [R_RR]),
            lambda: op("act", lambda e: e.activation(out=cosT[:, n0:n0 + nb, :], in_=rr, func=AF.Sin), [R_RR], [tabkey]),
        ]

    rot_part2 = []
    for n0_, nb_ in ((11, 8), (19, 8), (27, 6)):
        v3 = lambda apx, nb_=nb_: apx[:, 0:nb_ * 64].rearrange("p (n i) -> p n i", i=64)
        rot_part2 += rot_block_ops(n0_, nb_, v3(hc_sb[:, 0, :]), v3(hc_sb[:, 1, :]), v3(yc[:, 0, :].bitcast(I32)), v3(yc[:, 1, :]),
                                   [("hc_sb", 0), ("hc_sb", 1), ("yc", 0), ("yc", 1)], ("rot", 1))

    tables_built = [False]

    pe_ctr = {"pa": 0, "rot": 0, "sbf": 0, "wug": 0, "retb": 0, "dn": 0}
    pending = []
    wug_slot_of = {}
    tmpA_b = tmpA.bitcast(BF16)
    RTMP_ALL = [("tmpA", b, i) for b in range(2) for i in range(3)]

    def flush_pending(keep=0):
        while len(pending) > keep:
            pending.pop(0)()

    def norm_stages(t, s, m, wb_tile, wb_key, xslot=None, sq_qrot=False):
        c = stat_col(3)
        ssa = stat[:m, c:c + 1]; aa = stat[:m, c + 1:c + 2]; ra = stat[:m, c + 2:c + 3]
        RS = ("stat", c)
        hbi = s % 2
        xs_ = cur["xs"][s] if xslot is None else xslot

        def st0():
            if sq_qrot:
                op("act", lambda e: e.activation(out=qrot.rearrange("p a c -> p (a c)")[:m, :], in_=x_sb[:m, xs_, :], func=AF.Square, accum_out=ssa),
                   [("x", xs_), RS], [RS] + [("qrot", a, b) for a in range(2) for b in range(2)])
            else:
                op("act", lambda e: e.activation(out=hb[:m, hbi, :], in_=x_sb[:m, xs_, :], func=AF.Square, accum_out=ssa), [("x", xs_), RS], [RS, ("hb", hbi)])

        def st1():
            op("dve", lambda e: e.tensor_scalar(out=aa, in0=ssa, scalar1=1.0 / D, scalar2=EPS, op0=ALU.mult, op1=ALU.add), [RS], [RS])

        def st2():
            op("pool", lambda e: e.tensor_tensor(out=ra, in0=aa, in1=negh[:m, 0:1], op=ALU.pow), [RS, ("negh",)], [RS])

        def st3():
            op("dve", lambda e: e.scalar_tensor_tensor(out=hb[:m, hbi, :], in0=x_sb[:m, xs_, :], scalar=ra, in1=wb_tile[:m, :],
                                                       op0=ALU.mult, op1=ALU.mult),
               [("x", xs_), RS, wb_key], [("hb", hbi)])

        def part_b():
            RT = ("pT",)
            for k in range(8):
                op("pe", lambda e, k=k: e.transpose(out=pT[:, k * 128: k * 128 + m], in_=hb[:m, hbi, k * 128:(k + 1) * 128], identity=ident[:m, :m]),
                   [("hb", hbi), ("ident",)], [RT])
            src = pT.rearrange("p (k c) -> p k c", c=128)[:, :, :m]
            dst = hT[:, :, s * 128:s * 128 + m]
            op("dve", lambda e: e.tensor_copy(out=dst, in_=src), [RT], [("hT", s)])
        return [st0, st1, st2, st3, part_b]

    def norm_to_hT(t, s, m, wb_tile, wb_key):
        st = norm_stages(t, s, m, wb_tile, wb_key)
        for f in st[:4]:
            f()
        return st[4]

    def load_slab_into(g, slot_):
        dma("sp", wslab[:, slot_], win_s[g], [("win_s", g)], [("wslab", slot_)], ("wslab", slot_))
        return slot_

    def proj_tokmajor(t, g, slot, ns, msz, tabidx, pre_hooks=None):
        for s in range(ns):
            m = msz[s]
            if pre_hooks is not None:
                pre_hooks[s]()
            bank = pe_ctr["pa"] % 2; pe_ctr["pa"] += 1
            RB = ("pb", bank)
            for k in range(8):
                op("pe", lambda e, k=k, s=s, m=m, bank=bank: e.matmul(pb[bank][:m, :], lhsT=hT[:, k, s * 128:s * 128 + m],
                                                                       rhs=wslab[:, slot, k, :], start=(k == 0), stop=(k == 7)),
                   [("hT", s), ("wslab", slot)], [RB])
            flush_pending(keep=1)
            pq = pb[bank]
            if g in (0, 1):
                ti = tabidx[s]
                rb = pe_ctr["rot"] % 2; pe_ctr["rot"] += 1
                pqv = pq[:m, :].rearrange("p (h two d) -> p h two d", h=H, two=2)
                cb = cosT[:m, ti, :].unsqueeze(1).unsqueeze(1).broadcast_to([m, H, 2, 64])
                sbb = sinT[:m, ti, :].unsqueeze(1).broadcast_to([m, H, 64])
                t1 = tmpA_b[:m, rb * 1024:rb * 1024 + 512].rearrange("p (h two d) -> p h two d", h=H, two=2)
                t2 = tmpA_b[:m, rb * 1024 + 512:(rb + 1) * 1024].rearrange("p (h two d) -> p h two d", h=H, two=2)
                RT0, RT1, RT2 = ("tmpA", rb, 0), ("tmpA", rb, 1), ("tmpA", rb, 2)
                RTAB = ("rot", 0 if ti <= 10 else 1)
                op("dve", lambda e, pqv=pqv, cb=cb, t1=t1: e.tensor_tensor(out=t1, in0=pqv, in1=cb, op=ALU.mult), [RB, RTAB], [RT0])
                op("dve", lambda e, pqv=pqv, sbb=sbb, t2=t2: e.tensor_tensor(out=t2[:, :, 0, :], in0=pqv[:, :, 1, :], in1=sbb, op=ALU.mult),
                   [RB, RTAB], [RT1])
                op("dve", lambda e, pqv=pqv, sbb=sbb, t2=t2: e.tensor_tensor(out=t2[:, :, 1, :], in0=pqv[:, :, 0, :], in1=sbb, op=ALU.mult),
                   [RB, RTAB], [RT2])
                qi = rb
                qr = qrot[:m, qi, :].rearrange("p (h two d) -> p h two d", h=H, two=2)
                RQ0, RQ1 = ("qrot", qi, 0), ("qrot", qi, 1)
                op("pool", lambda e, t1=t1, t2=t2, qr=qr: e.tensor_tensor(out=qr[:, :, 0, :], in0=t1[:, :, 0, :], in1=t2[:, :, 0, :], op=ALU.subtract),
                   [RT0, RT1], [RQ0])
                op("pool", lambda e, t1=t1, t2=t2, qr=qr: e.tensor_tensor(out=qr[:, :, 1, :], in0=t1[:, :, 1, :], in1=t2[:, :, 1, :], op=ALU.add),
                   [RT0, RT2], [RQ1])
                if g == 1:
                    tix = 0 if t == 0 else 1
                    teb = te_t[:m, tix, :].unsqueeze(2).broadcast_to([m, H, 128])
                    kdst = kte[:m, s, :].rearrange("p (h d) -> p h d", h=H)
                    qsrc = qrot[:m, qi, :].rearrange("p (h d) -> p h d", h=H)
                    op("pool", lambda e, kdst=kdst, qsrc=qsrc, teb=teb: e.tensor_tensor(out=kdst, in0=qsrc, in1=teb, op=ALU.mult),
                       [RQ0, RQ1, ("te_t",)], [("kte", s)])
                    tsrc_all = kte[:m, s, :]
                    RSRC = [("kte", s)]
                    dstT = kT
                    RD = ("kT", s)
                else:
                    tsrc_all = qrot[:m, qi, :]
                    RSRC = [RQ0, RQ1]
                    dstT = qT
                    RD = ("qT", s)

                def tr(tsrc_all=tsrc_all, RSRC=RSRC, dstT=dstT, RD=RD, m=m, s=s):
                    RT = ("pT",)
                    for h in range(H):
                        op("pe", lambda e, h=h: e.transpose(out=pT[:, h * 128: h * 128 + m], in_=tsrc_all[:, h * 128:(h + 1) * 128], identity=ident[:m, :m]),
                           RSRC + [("ident",)], [RT])
                    src = pT[:, 0:512].rearrange("p (k c) -> p k c", c=128)[:, :, :m]
                    dst = dstT[:, :, s * 128:s * 128 + m]
                    op("act", lambda e: e.activation(out=dst, in_=src, func=AF.Copy), [RT], [RD])
                pending.append(tr)
            elif g == 2:
                op("act", lambda e, pq=pq, s=s, m=m: e.activation(out=v_sb[:m, s, :], in_=pq[:m, :], func=AF.Copy), [RB], [("v_sb", s)])
            else:
                op("act", lambda e, pq=pq, s=s, m=m: e.activation(out=gs[:m, s, :], in_=pq[:m, :], func=AF.Silu), [RB], [("gs", s)])
                op("pool", lambda e, s=s, m=m: e.tensor_tensor(out=gs[:m, s, :], in0=gs[:m, s, :], in1=wb_ret[:m, :], op=ALU.mult),
                   [("gs", s), ("wb_ret",)], [("gs", s)])

    def retention_parts(t, s, m):
        tix = 0 if t == 0 else 1
        cols = slice(s * 128, s * 128 + m)
        R3, R4, R5 = ("pb", 3), ("pb", 4), ("pb", 5)
        sb_cur = pe_ctr["sbf"] % 2; pe_ctr["sbf"] += 1
        sb_nxt = 1 - sb_cur
        c = stat_col(12)
        ssh = stat[:m, c:c + 4]; ah = stat[:m, c + 4:c + 8]; rh = stat[:m, c + 8:c + 12]
        RS = ("stat", c)
        rbi = pe_ctr["retb"] % 2; pe_ctr["retb"] += 1

        def part_a():
            for h in range(H):
                op("pe", lambda e, h=h: e.matmul(pb[3][:m, h * 128:h * 128 + m], lhsT=kT[:, h, cols], rhs=qT[:, h, cols], start=True, stop=True),
                   [("kT", s), ("qT", s)], [R3])
            for h in range(H):
                op("pe", lambda e, h=h: e.matmul(pb[5][:, h * 128:(h + 1) * 128], lhsT=kte[:m, s, h * 128:(h + 1) * 128], rhs=v_sb[:m, s, h * 128:(h + 1) * 128],
                                                 start=True, stop=True), [("kte", s), ("v_sb", s)], [R5])
            scv = pb[3][:m, :].rearrange("p (h c) -> p h c", h=H)[:, :, :m]
            op("dve", lambda e: e.tensor_tensor(out=Pm[:m, :, :m], in0=scv, in1=(mask2s if t == 0 else mask2)[:m, :, :m], op=ALU.mult), [R3, ("mask2",)], [("Pm",)])
            for h in range(H):
                gval = math.exp(LG[h] * (16 if t == 0 else 128))
                op("dve", lambda e, h=h, gval=gval: e.scalar_tensor_tensor(out=S[:, h, :], in0=S[:, h, :], scalar=gval, in1=pb[5][:, h * 128:(h + 1) * 128],
                                                                           op0=ALU.mult, op1=ALU.add), [("S", h), R5], [("S", h)])
            op("dve", lambda e: e.tensor_copy(out=S_bf[:, sb_nxt], in_=S), [("S", h) for h in range(H)], [("S_bf", sb_nxt)])

        def part_b():
            for h in range(H):
                op("pe", lambda e, h=h: e.matmul(pb[4][:m, h * 128:(h + 1) * 128], lhsT=Pm[:m, h, :m], rhs=v_sb[:m, s, h * 128:(h + 1) * 128],
                                                 start=True, stop=False), [("Pm",), ("v_sb", s)], [R4])
                op("pe", lambda e, h=h: e.matmul(pb[4][:m, h * 128:(h + 1) * 128], lhsT=qT[:, h, cols], rhs=S_bf[:, sb_cur, h, :],
                                                 start=False, stop=True), [("qT", s), ("S_bf", sb_cur)], [R4])
            for h in range(H):
                op("act", lambda e, h=h: e.activation(out=ret_b[:m, rbi, h * 128:(h + 1) * 128], in_=pb[4][:m, h * 128:(h + 1) * 128], func=AF.Square,
                                                      accum_out=ssh[:, h:h + 1]), [R4, RS], [RS, ("ret_b", rbi, h)])
            op("dve", lambda e: e.scalar_tensor_tensor(out=ah, in0=ssh, scalar=1.0 / 128.0, in1=epsfs[:m, tix, :], op0=ALU.mult, op1=ALU.add),
               [RS, ("epsfs",)], [RS])
            op("pool", lambda e: e.tensor_tensor(out=rh, in0=ah, in1=negh[:m, 0:4], op=ALU.pow), [RS, ("negh",)], [RS])

        def part_c():
            for h in range(H):
                op("dve", lambda e, h=h: e.scalar_tensor_tensor(out=ret_b[:m, rbi, h * 128:(h + 1) * 128], in0=pb[4][:m, h * 128:(h + 1) * 128],
                                                                scalar=rh[:, h:h + 1], in1=gs[:m, s, h * 128:(h + 1) * 128], op0=ALU.mult, op1=ALU.mult),
                   [R4, RS, ("gs", s)], [("ret_b", rbi, h)])

            def tr():
                RT = ("pT",)
                for h in range(H):
                    op("pe", lambda e, h=h: e.transpose(out=pT[:, h * 128: h * 128 + m], in_=ret_b[:m, rbi, h * 128:(h + 1) * 128],
                                                        identity=ident[:m, :m]), [("ret_b", rbi, h), ("ident",)], [RT])
                src = pT[:, 0:512].rearrange("p (k c) -> p k c", c=128)[:, :, :m]
                op("act", lambda e: e.activation(out=mixT[:, 0:4, cols], in_=src, func=AF.Copy), [RT], [("mixTr", s)])
            pending.append(tr)
        return part_a, part_b, part_c

    def post_stages(src_ap, src_reads, s, m, wb_tile, wb_key):
        c = stat_col(3)
        ssa = stat[:m, c:c + 1]; aa = stat[:m, c + 1:c + 2]; ra = stat[:m, c + 2:c + 3]
        RS = ("stat", c)
        xs_ = cur["xs"][s]

        def p0():
            op("act", lambda e: e.activation(out=tmpA_b[:m, 0:1024], in_=src_ap, func=AF.Square, accum_out=ssa), list(src_reads) + [RS], [RS] + RTMP_ALL)

        def p1():
            op("dve", lambda e: e.tensor_scalar(out=aa, in0=ssa, scalar1=1.0 / D, scalar2=EPS, op0=ALU.mult, op1=ALU.add), [RS], [RS])

        def p2():
            op("pool", lambda e: e.tensor_tensor(out=ra, in0=aa, in1=negh[:m, 0:1], op=ALU.pow), [RS, ("negh",)], [RS])

        def p3():
            op("dve", lambda e: e.scalar_tensor_tensor(out=tmpA[:m, :], in0=src_ap, scalar=ra, in1=wb_tile[:m, :], op0=ALU.mult, op1=ALU.mult),
               list(src_reads) + [RS, wb_key], RTMP_ALL)

        def p4():
            op("dve", lambda e: e.tensor_tensor(out=x_sb[:m, xs_, :], in0=x_sb[:m, xs_, :], in1=tmpA[:m, :], op=ALU.add), RTMP_ALL + [("x", xs_)], [("x", xs_)])
        return [p0, p1, p2, p3, p4]

    def post_norm_residual(src_ap, src_reads, s, m, wb_tile, wb_key):
        for f_ in post_stages(src_ap, src_reads, s, m, wb_tile, wb_key):
            f_()

    def load_wug(t, c):
        wslot = pe_ctr["wug"] % NWUG; pe_ctr["wug"] += 1
        wug_slot_of[(t, c)] = wslot
        dma("sp", wug_sb[:, wslot], wug_s[c], [("wug_s", c, 0), ("wug_s", c, 1)], [("wug", wslot)], ("wug", wslot))

    def featmajor_mm(slot_, j, nt):
        bank = pe_ctr["pa"] % 2; pe_ctr["pa"] += 1
        for k in range(8):
            op("pe", lambda e, k=k: e.matmul(pb[bank][:, :nt], lhsT=wslab[:, slot_, k, j * 128:(j + 1) * 128],
                                             rhs=hT[:, k, :nt], start=(k == 0), stop=(k == 7)),
               [("hT", s) for s in range(4)] + [("wslab", slot_)], [("pb", bank)])
        return bank

    def conv_ops(j, nt):
        yi = j % 2
        zb, zk = cur["z"], cur["zk"]
        t1 = tmpA[:, 0:nt]
        t0 = tmpA[:, 512:512 + nt]
        R1 = [("tmpA", 0, i) for i in range(3)]
        R0 = [("tmpA", 1, i) for i in range(3)]
        op("act", lambda e: e.activation(out=yc[:, yi, :nt], in_=zb[:, j, 2:2 + nt], func=AF.Identity, scale=cw[:, j, 2:3]), [(zk, j), ("cw",)], [("yc", yi)])
        op("act", lambda e: e.activation(out=t1, in_=zb[:, j, 1:1 + nt], func=AF.Identity, scale=cw[:, j, 1:2]), [(zk, j), ("cw",)], R1)
        op("act", lambda e: e.activation(out=t0, in_=zb[:, j, 0:nt], func=AF.Identity, scale=cw[:, j, 0:1]), [(zk, j), ("cw",)], R0)
        op("pool", lambda e: e.tensor_tensor(out=yc[:, yi, :nt], in0=yc[:, yi, :nt], in1=t1, op=ALU.add), R1 + [("yc", yi)], [("yc", yi)])
        op("pool", lambda e: e.tensor_tensor(out=yc[:, yi, :nt], in0=yc[:, yi, :nt], in1=t0, op=ALU.add), R0 + [("yc", yi)], [("yc", yi)])
        op("pool", lambda e: e.tensor_copy(out=zb[:, j, 0:2], in_=zb[:, j, nt:nt + 2]), [(zk, j)], [(zk, j)])

    def b_chunk(slot_b, j, nt, mid=None):
        yi = j % 2
        bank = featmajor_mm(slot_b, j, nt)
        if mid is not None:
            mid()
        op("dve", lambda e: e.tensor_tensor(out=mixT[:, 4 + j, :nt], in0=pb[bank][:, :nt], in1=yc[:, yi, :nt], op=ALU.mult),
           [("pb", bank), ("yc", yi)], [("mixT", 4 + j)])

    tile_order = list(range(1, 1 + NT_PROMPT)) + [0]
    prefetched = {}

    def tile_params(t):
        i = tile_order.index(t)
        if t == 0:
            nt, ns, msz, tabidx = DEC, 1, [DEC], [0]
            xsrc = [xs]
            ydst = [ys]
        else:
            nt, ns, msz = TT, 4, [128] * 4
            base = (t - 1) * TT
            tabidx = [1 + (t - 1) * 4 + s for s in range(4)]
            xsrc = [xp[base + s * 128: base + (s + 1) * 128, :] for s in range(4)]
            ydst = [yp[base + s * 128: base + (s + 1) * 128, :] for s in range(4)]
        slots = [(4 * i + s) % 5 for s in range(ns)]
        return dict(t=t, nt=nt, ns=ns, msz=msz, tabidx=tabidx, xsrc=xsrc, ydst=ydst, slots=slots)

    def load_x(tp, s):
        sl = tp["slots"][s]
        dma("sp", x_sb[:tp["msz"][s], sl, :], tp["xsrc"][s], [], [("x", sl)], ("x", sl))
        prefetched[("x", tp["t"], s)] = True

    n1_recs = {}

    def n1_rec(t, s, tp):
        if (t, s) not in n1_recs:
            n1_recs[(t, s)] = {"st": norm_stages(t, s, tp["msz"][s], wb_pre_mix, ("wb_pre_mix",), xslot=tp["slots"][s], sq_qrot=(s >= 2)), "done": 0}
        return n1_recs[(t, s)]

    def adv(rec, upto):
        while rec["done"] < upto:
            rec["st"][rec["done"]]()
            rec["done"] += 1

    def run_tile(t):
        tp = tile_params(t)
        nt, ns, msz, tabidx, ydst = tp["nt"], tp["ns"], tp["msz"], tp["tabidx"], tp["ydst"]
        i = tile_order.index(t)
        tp_next = tile_params(tile_order[i + 1]) if i + 1 < len(tile_order) else None
        if t == 0:
            cur.update(z=z_s, zk="z_s", hf=halo_s, hfk="halo_s")
        else:
            cur.update(z=z, zk="z", hf=halo_f, hfk="halo_f")
        cur["xs"] = tp["slots"]
        for s in range(ns):
            if not prefetched.get(("x", t, s)):
                load_x(tp, s)
        recs = [n1_rec(t, s, tp) for s in range(ns)]
        for p0 in range(0, ns, 2):
            grp = list(range(p0, min(p0 + 2, ns)))
            for stage in range(1, 4):
                for s in grp:
                    adv(recs[s], stage)
            if p0 == 0:
                issue_casts(8)
        slot = 0 if prefetched.get(("slab", t)) else load_slab_into(0, 0)
        for s in range(min(2, ns)):
            adv(recs[s], 4)
        if not tables_built[0]:
            tables_built[0] = True
            build_tables()
        n1_hooks = {}
        for s in range(ns):
            def hook(s=s):
                adv(recs[s], 5)
                if s + 2 < ns:
                    adv(recs[s + 2], 4)
            n1_hooks[s] = hook
        order = [0, 1, 2, 3]
        for gi, g in enumerate(order):
            nxt = order[gi + 1] if gi + 1 < 4 else 6
            if gi == 0 and prefetched.get(("slab1", t)):
                nslot = 1 - slot
            else:
                nslot = load_slab_into(nxt, 1 - slot)
            proj_tokmajor(t, g, slot, ns, msz, tabidx, n1_hooks if gi == 0 else None)
            slot = nslot
            issue_casts(6)
        flush_pending()
        slot_hc = slot
        slot_c = load_slab_into(5, 1 - slot_hc)
        dma("sp", wout_sb, wout_s, [("wout_s",)], [("facc", s, hh) for s in range(4) for hh in range(2)], ("wout",))
        for j in range(4):
            bank = featmajor_mm(slot_hc, j, nt)
            hi = j % 2
            op("act", lambda e, bank=bank, hi=hi: e.activation(out=hc_sb[:, hi, :nt], in_=pb[bank][:, :nt], func=AF.Copy), [("pb", bank)], [("hc_sb", hi)])
            bank2 = featmajor_mm(slot_c, j, nt)
            op("dve", lambda e, bank2=bank2, hi=hi, j=j, zb=cur["z"]: e.tensor_tensor(out=zb[:, j, 2:2 + nt], in0=pb[bank2][:, :nt], in1=hc_sb[:, hi, :nt], op=ALU.mult),
               [("pb", bank2), ("hc_sb", hi)], [(cur["zk"], j)])
        slot_b = load_slab_into(4, slot_hc)
        issue_casts(6)
        load_wug(t, 0)
        load_wug(t, 1)
        dma("sp", wdn_sb, wdn_s[0], [("wdn_s", 0)], [("wdn_sb",)], ("wdn",))
        parts = [retention_parts(t, s, msz[s]) for s in range(ns)]
        parts[0][0]()
        parts[0][1]()
        if ns == 4:
            for s in range(ns):
                if s + 1 < ns:
                    parts[s + 1][0]()
                conv_ops(s, nt)
                b_chunk(slot_b, s, nt, mid=parts[s][2])
                flush_pending(keep=1)
                if s + 1 < ns:
                    parts[s + 1][1]()
            flush_pending()
        else:
            parts[0][2]()
            for j in range(4):
                conv_ops(j, nt)
                b_chunk(slot_b, j, nt)
            flush_pending()
        def wout_mm(s):
            m = msz[s]
            cols = slice(s * 128, s * 128 + m)
            if s % 2 == 0:
                pbuf, regs = p67, [("p67", 0), ("p67", 1)]
            else:
                pbuf, regs = p34, [("pb", 3), ("pb", 4)]
            for half in range(2):
                for k in range(8):
                    rk = ("mixTr", s) if k < 4 else ("mixT", k)
                    op("pe", lambda e, k=k, half=half: e.matmul(pbuf[:m, half * 512:(half + 1) * 512], lhsT=mixT[:, k, cols],
                                                                rhs=wout_sb[:, k, half * 512:(half + 1) * 512], start=(k == 0), stop=(k == 7)),
                       [rk] + [("facc", q, hh) for q in range(4) for hh in range(2)], [regs[half]])
            return post_stages(pbuf[:m, :], regs, s, m, wb_post_mix, ("wb_post_mix",))
        n2b = {}
        for s in range(ns + 2):
            post = wout_mm(s) if s < ns else []
            n2 = norm_stages(t, s - 1, msz[s - 1], wb_pre_ffn, ("wb_pre_ffn",)) if 1 <= s <= ns else []
            for k in range(5):
                if k < len(post):
                    post[k]()
                if n2 and k < 4:
                    n2[k]()
            if n2:
                n2b[s - 1] = n2[4]
            if 2 <= s:
                n2b[s - 2]()
        issue_casts(6)
        if tp_next is not None:
            load_x(tp_next, 0)
            load_slab_into(0, 0)
            prefetched[("slab", tile_order[i + 1])] = True
            load_slab_into(1, 1)
            prefetched[("slab1", tile_order[i + 1])] = True
        def emit_down(g):
                if g == 3 and tp_next is not None:
                    adv(n1_rec(tile_order[i + 1], 0, tp_next), 5)
                for s in range(ns):
                    m = msz[s]
                    cols = slice(s * 128, s * 128 + m)
                    for half in range(2):
                        di = pe_ctr["dn"] % 4; pe_ctr["dn"] += 1
                        RH, dbank = [(("p67", 0), p67[:, 0:512]), (("p67", 1), p67[:, 512:1024]), (("pb", 3), pb[3]), (("pb", 4), pb[4])][di]
                        hs = slice(half * 512, (half + 1) * 512)
                        for cc in range(8):
                            op("pe", lambda e, cc=cc, hs=hs, m=m, cols=cols, dbank=dbank: e.matmul(dbank[:m, :], lhsT=act[:, cc, cols], rhs=wdn_sb[:, cc, hs],
                                                                                                     start=(cc == 0), stop=(cc == 7)),
                               [("act", cc), ("wdn_sb",)], [RH])
                        if g == 0:
                            op("act", lambda e, s=s, m=m, hs=hs, dbank=dbank: e.activation(out=f_acc[:m, s, hs], in_=dbank[:m, :], func=AF.Copy), [RH], [("facc", s, half)])
                        else:
                            op("dve", lambda e, s=s, m=m, hs=hs, dbank=dbank: e.tensor_tensor(out=f_acc[:m, s, hs], in0=f_acc[:m, s, hs], in1=dbank[:m, :], op=ALU.add),
                               [RH, ("facc", s, half)], [("facc", s, half)])
                    if g == 3:
                        post = post_stages(f_acc[:m, s, :], [("facc", s, 0), ("facc", s, 1)], s, m, wb_post_ffn, ("wb_post_ffn",))
                        nrec = None
                        if tp_next is not None:
                            tn = tile_order[i + 1]
                            if 1 <= s - 2 < tp_next["ns"]:
                                adv(n1_rec(tn, s - 2, tp_next), 5)
                            if 1 <= s - 1 < tp_next["ns"]:
                                nrec = n1_rec(tn, s - 1, tp_next)
                        for k in range(5):
                            post[k]()
                            if nrec is not None and k < 4:
                                adv(nrec, k + 1)
                        dma("sp", ydst[s], x_sb[:m, tp["slots"][s], :], [("x", tp["slots"][s])], [], ("y", tp["slots"][s]), is_out=True)
                        if tp_next is not None and s + 1 < tp_next["ns"]:
                            load_x(tp_next, s + 1)

        for g in range(4):
            if g == 3 and tp_next is not None:
                adv(n1_rec(tile_order[i + 1], 0, tp_next), 4)
            for cc in range(8):
                c = g * 8 + cc
                if g > 0 and cc == 0 and t == 0:
                    emit_down(g - 1)
                wslot = wug_slot_of[(t, c)]
                if c + 2 < 32:
                    load_wug(t, c + 2)
                for _ in range(2):
                    if rot_part2:
                        rot_part2.pop(0)()
                if g > 0 and cc == 1:
                    dma("sp", wdn_sb, wdn_s[g], [("wdn_s", g)], [("wdn_sb",)], ("wdn",))
                issue_casts(2 if cc else 3)
                if t == 0:
                    b3 = c % 3
                    PU, RU, PG, RG = [(pb[0], ("pb", 0), pb[1], ("pb", 1)), (pb[3], ("pb", 3), pb[4], ("pb", 4)),
                                      (pb[5], ("pb", 5), p67[:, 0:512], ("p67", 0))][b3]
                    if b3 < 2:
                        upv, RUP = up_sb[:, b3, :], ("upsb", b3)
                        accv, RACC = acc[:, b3, :], ("acc", b3)
                        glv, RGL = gl[:, b3, :], ("gl", b3)
                    else:
                        upv, RUP = hc_sb[:, 0, :], ("hc_sb", 0)
                        accv, RACC = hc_sb[:, 1, :], ("hc_sb", 1)
                        glv, RGL = yc.bitcast(BF16)[:, 0, :], ("yc", 0)
                else:
                    b = c % 2
                    PU, RU, PG, RG = [(pb[0], ("pb", 0), pb[1], ("pb", 1)), (pb[3], ("pb", 3), pb[4], ("pb", 4))][b]
                    upv, RUP = up_sb[:, b, :], ("upsb", b)
                    accv, RACC = acc[:, b, :], ("acc", b)
                    glv, RGL = gl[:, b, :], ("gl", b)
                hf, hfk = cur["hf"], cur["hfk"]
                for k in range(8):
                    op("pe", lambda e, k=k, wslot=wslot, PU=PU: e.matmul(PU[:, :nt], lhsT=wug_sb[:, wslot, 0, k, :], rhs=hT[:, k, :nt],
                                                                          start=(k == 0), stop=(k == 7)),
                       [("hT", s) for s in range(4)] + [("wug", wslot)], [RU])
                for k in range(8):
                    op("pe", lambda e, k=k, wslot=wslot, PG=PG: e.matmul(PG[:, :nt], lhsT=wug_sb[:, wslot, 1, k, :], rhs=hT[:, k, :nt],
                                                                          start=(k == 0), stop=(k == 7)),
                       [("hT", s) for s in range(4)] + [("wug", wslot)], [RG])
                if g > 0 and cc == 0 and t != 0:
                    emit_down(g - 1)
                op("pool", lambda e, upv=upv, c=c, hf=hf: e.tensor_copy(out=upv[:, 0:2], in_=hf[:, c, :]), [(hfk,)], [RUP])
                op("act", lambda e, upv=upv, PU=PU: e.activation(out=upv[:, 2:2 + nt], in_=PU[:, :nt], func=AF.Copy), [RU, RUP], [RUP])
                op("pool", lambda e, upv=upv, c=c, hf=hf: e.tensor_copy(out=hf[:, c, :], in_=upv[:, nt:nt + 2]), [RUP, (hfk,)], [(hfk,)])
                op("act", lambda e, accv=accv, PU=PU, c=c: e.activation(out=accv[:, :nt], in_=PU[:, :nt], func=AF.Identity, scale=fw[:, c, 2:3]),
                   [RU, ("fw",)], [RACC])
                op("dve", lambda e, accv=accv, upv=upv, c=c: e.scalar_tensor_tensor(out=accv[:, :nt], in0=upv[:, 1:1 + nt], scalar=fw[:, c, 1:2], in1=accv[:, :nt],
                                                                                    op0=ALU.mult, op1=ALU.add), [RUP, RACC, ("fw",)], [RACC])
                op("dve", lambda e, accv=accv, upv=upv, c=c: e.scalar_tensor_tensor(out=accv[:, :nt], in0=upv[:, 0:nt], scalar=fw[:, c, 0:1], in1=accv[:, :nt],
                                                                                    op0=ALU.mult, op1=ALU.add), [RUP, RACC, ("fw",)], [RACC])
                op("act", lambda e, accv=accv, glv=glv: e.activation(out=glv[:, :nt], in_=accv[:, :nt], func=AF.Gelu_apprx_tanh), [RACC], [RGL])
                op("dve", lambda e, glv=glv, PG=PG, cc=cc: e.tensor_tensor(out=act[:, cc, :nt], in0=PG[:, :nt], in1=glv[:, :nt], op=ALU.mult),
                   [RG, RGL], [("act", cc)])
        emit_down(3)

    def emit_cache_outputs(cc_o, cf_o, zb, zk, hf, hfk, tag):
        R5 = ("pb", 5)
        for r in range(2):
            op("pe", lambda e, r=r: e.transpose(out=pb[5][:32, r * 128:(r + 1) * 128], in_=hf[:, :, r], identity=ident_f),
               [(hfk,), ("ident_f",)], [R5])
            op("pe", lambda e, r=r: e.transpose(out=pb[5][:4, (2 + r) * 128:(3 + r) * 128], in_=zb[:, :, r], identity=ident_f),
               [(zk, j) for j in range(4)] + [("ident_f",)], [R5])
        op("act", lambda e: e.activation(out=tmpA[:32, 0:512], in_=pb[5][:32, :], func=AF.Copy), [R5], RTMP_ALL)
        for r in range(2):
            dma("sp", cf_o[r].rearrange("(c p) -> c p", p=128), tmpA[:32, r * 128:(r + 1) * 128], RTMP_ALL, [], ("so", tag), is_out=True)
            dma("sp", cc_o[r].rearrange("(j p) -> j p", p=128), tmpA[:4, (2 + r) * 128:(3 + r) * 128], RTMP_ALL, [], ("so", tag), is_out=True)

    SALL = [("S", h) for h in range(H)]
    op("dve", lambda e: e.memset(S, 0.0), [], SALL)
    op("dve", lambda e: e.memset(S_bf, 0.0), [], [("S_bf", 0), ("S_bf", 1)])
    op("dve", lambda e: e.memset(z[:, :, 0:2], 0.0), [], [("z", j) for j in range(4)])
    op("dve", lambda e: e.memset(halo_f, 0.0), [], [("halo_f",)])
    for r in range(2):
        dma("sp", z_s[:, :, r], cc_in[r].rearrange("(j p) -> p j", p=128), [], [("z_s", j) for j in range(4)], ("cstslow",), slow=True)
    for r in range(2):
        dma("sp", halo_s[:, :, r], cf_in[r].rearrange("(c p) -> p c", p=128), [], [("halo_s",)], ("cstslow",), slow=True)
    for t in tile_order[:-1]:
        run_tile(t)
    dma("sp", srp.rearrange("h d e -> d h e"), S, SALL, [], ("so", "p0"), is_out=True)
    dma("sp", S, st_in.rearrange("h d e -> d h e"), [], SALL, ("cst2",))
    sbn = pe_ctr["sbf"] % 2
    op("dve", lambda e: e.tensor_copy(out=S_bf[:, sbn], in_=S), SALL, [("S_bf", sbn)])
    run_tile(0)
    dma("sp", srs.rearrange("h d e -> d h e"), S, SALL, [], ("so", "s0"), is_out=True)
    emit_cache_outputs(ccp, cfp, z, "z", halo_f, "halo_f", "p")
    emit_cache_outputs(ccs, cfs, z_s, "z_s", halo_s, "halo_s", "s")
    P.emit()
    return nc


_CACHE = {}


def kernel(x_prompt, x_sample, state_ret, cache_conv, cache_ffn_conv,
           w_in, w_out, conv_w, ret_norm_w, pre_mix_w, post_mix_w,
           pre_ffn_w, post_ffn_w, w_up, w_gate, ffn_conv_w, w_down):
    f = lambda a: np.ascontiguousarray(np.asarray(a, dtype=np.float32))
    x_prompt = f(x_prompt); x_sample = f(x_sample); state_ret = f(state_ret)
    cache_conv = f(cache_conv); cache_ffn_conv = f(cache_ffn_conv)
    shared = {
        "w_in": f(w_in)[0], "w_out": f(w_out)[0], "conv_w": f(conv_w)[0], "ret_w": f(ret_norm_w),
        "pre_mix": f(pre_mix_w), "post_mix": f(post_mix_w), "pre_ffn": f(pre_ffn_w), "post_ffn": f(post_ffn_w),
        "w_up": f(w_up)[0], "w_gate": f(w_gate)[0], "ffn_cw": f(ffn_conv_w)[0], "w_down": f(w_down)[0],
    }
    if "nc" not in _CACHE:
        _CACHE["nc"] = build_program()
    nc = _CACHE["nc"]
    in_maps = []
    for c in range(NCORES):
        m = dict(shared)
        m["xp"] = x_prompt[c]; m["xs"] = x_sample[c]
        m["st"] = state_ret[0, c]; m["cc"] = cache_conv[0, c]; m["cf"] = cache_ffn_conv[0, c]
        in_maps.append(m)
    res = run_bass_kernel_spmd(nc, in_maps, core_ids=list(range(NCORES)))
    r = res.results
    st = lambda k: np.stack([np.asarray(r[c][k], dtype=np.float32) for c in range(NCORES)])
    yp = st("yp"); ys = st("ys")
    return (yp, ys, st("srp")[None], st("ccp")[None], st("cfp")[None],
            st("srs")[None], st("ccs")[None], st("cfs")[None])
```
